# Optimizing a Trainium2 kernel written in Bass

```python
import math
import jax
import jax.numpy as jnp
from jax import lax
import numpy as np

D_MODEL = 2048
BATCH = 2
SEQ = 16384
DEPTH = 2
DEC_BATCH = 8
DEC_SEQ = 64
PAST_LEN = 4096

CHUNK = 64
N_META = 16
HEAD_DIM = 128
DA_HEADS = 4
DA_DK = HEAD_DIM
DA_DV = 2 * HEAD_DIM
GDN_HEADS = 4
GDN_DK = HEAD_DIM
GDN_DV = HEAD_DIM
GDN_CONV = 4
ML_HEADS = 4
ML_DK = HEAD_DIM
ML_DV = HEAD_DIM
D_MIX = DA_HEADS * DA_DV + GDN_HEADS * GDN_DV + ML_HEADS * ML_DV
FFN_DIM = 5632
FFN_CONV = 3
NUM_BUCKETS = 32
MAX_DISTANCE = 128
Q_BLOCK = 128
RMS_EPS = 1e-6
NEG_INF = -1e30
DA_QK_COLS = DA_HEADS * 2 * DA_DK
DA_V_COLS = DA_HEADS * DA_DV
GDN_QKV_COLS = GDN_HEADS * (2 * GDN_DK + GDN_DV)
ML_QKV_COLS = ML_HEADS * (2 * ML_DK + ML_DV)
IN_SIZES = (DA_QK_COLS, DA_QK_COLS, DA_V_COLS, GDN_QKV_COLS, GDN_HEADS, GDN_HEADS, GDN_HEADS * GDN_DV, ML_QKV_COLS, ML_HEADS, ML_HEADS, ML_HEADS * ML_DV)
IN_COLS = sum(IN_SIZES)

kernel_name = 'hybrid_diffattn_gdn_mlstm_stream_step'


def rmsnorm(x, w):
    xf = x.astype(jnp.float32)
    y = xf * lax.rsqrt(jnp.mean(xf * xf, axis=-1, keepdims=True) + RMS_EPS)
    return (y * w.astype(jnp.float32)).astype(x.dtype)


def l2norm(x):
    xf = x.astype(jnp.float32)
    return xf * lax.rsqrt(jnp.sum(xf * xf, axis=-1, keepdims=True) + RMS_EPS)


def split_cols(z):
    parts, start = [], 0
    for size in IN_SIZES:
        parts.append(z[..., start:start + size])
        start += size
    return parts


def causal_dwconv(x, prev, w):
    width = w.shape[0]
    L = x.shape[1]
    xp = jnp.concatenate([prev.astype(x.dtype), x], axis=1)
    y = sum(xp[:, i:i + L] * w[i].astype(x.dtype) for i in range(width))
    return y, xp[:, xp.shape[1] - (width - 1):]


def prompt_chunk_ids(pos):
    return jnp.where(pos < N_META, 0, 1 + (pos - N_META) // CHUNK)


def t5_bucket(rel):
    half = NUM_BUCKETS // 2
    exact = half // 2
    n = jnp.abs(rel)
    nf = jnp.maximum(n, 1).astype(jnp.float32)
    large = exact + (jnp.log(nf / exact) / math.log(MAX_DISTANCE / exact) * (half - exact)).astype(jnp.int32)
    large = jnp.minimum(large, half - 1)
    return jnp.where(rel > 0, half, 0) + jnp.where(n < exact, n, large)


def diff_attend(q, q_pos, q_chunk, k, v, k_pos, k_chunk, rel_bias, lam):
    s = jnp.einsum('bqhmd,bkhmd->bhmqk', q, k).astype(jnp.float32) * (DA_DK ** -0.5)
    bias = rel_bias.astype(jnp.float32)[t5_bucket(k_pos[None, :] - q_pos[:, None])]
    s = s + jnp.transpose(bias, (2, 0, 1))[None, :, None]
    visible = k_chunk[None, :] <= q_chunk[:, None]
    s = jnp.where(visible[None, None, None], s, NEG_INF)
    p = jax.nn.softmax(s, axis=-1)
    a = p[:, :, 0] - lam * p[:, :, 1]
    return jnp.einsum('bhqk,bkhd->bqhd', a.astype(v.dtype), v)


def gdn_chunk(S, c):
    q, k, v, beta, g = c
    q, k, v = jnp.swapaxes(q, 1, 2), jnp.swapaxes(k, 1, 2), jnp.swapaxes(v, 1, 2)
    beta = jnp.swapaxes(beta, 1, 2)
    G = jnp.cumsum(jnp.swapaxes(g, 1, 2), axis=-1)
    L = q.shape[2]
    incl = jnp.tril(jnp.ones((L, L), dtype=bool))
    strict = jnp.tril(jnp.ones((L, L), dtype=bool), -1)
    diff = G[..., :, None] - G[..., None, :]
    dec = jnp.where(incl, jnp.exp(jnp.where(incl, diff, 0.0)), 0.0)
    kk = jnp.einsum('bhld,bhmd->bhlm', k, k)
    a_mat = jnp.eye(L, dtype=jnp.float32) + jnp.where(strict, beta[..., :, None] * kk * dec, 0.0)
    rhs = beta[..., None] * (v - jnp.exp(G)[..., None] * jnp.einsum('bhld,bhdv->bhlv', k, S))
    u = lax.linalg.triangular_solve(a_mat, rhs, left_side=True, lower=True, unit_diagonal=True)
    qk = jnp.einsum('bhld,bhmd->bhlm', q, k) * dec
    o = jnp.exp(G)[..., None] * jnp.einsum('bhld,bhdv->bhlv', q, S) + jnp.einsum('bhlm,bhmv->bhlv', qk, u)
    g_last = G[..., -1:]
    S_new = jnp.exp(g_last)[..., None] * S + jnp.einsum('bhl,bhld,bhlv->bhdv', jnp.exp(g_last - G), k, u)
    return S_new, jnp.swapaxes(o, 1, 2)


def mlstm_chunk(state, c):
    C, n, m = state
    q, k, v, ig, lf = c
    q, k, v = jnp.swapaxes(q, 1, 2), jnp.swapaxes(k, 1, 2), jnp.swapaxes(v, 1, 2)
    ig, lf = jnp.swapaxes(ig, 1, 2), jnp.swapaxes(lf, 1, 2)
    L = q.shape[2]
    incl = jnp.tril(jnp.ones((L, L), dtype=bool))
    b = jnp.cumsum(lf, axis=-1)
    D = jnp.where(incl, b[..., :, None] - b[..., None, :] + ig[..., None, :], NEG_INF)
    inter = b + m[..., None]
    m_t = jnp.maximum(inter, jnp.max(D, axis=-1))
    w_intra = jnp.exp(D - m_t[..., None])
    w_inter = jnp.exp(inter - m_t)
    qk = jnp.einsum('bhld,bhmd->bhlm', q, k) * w_intra
    num = w_inter[..., None] * jnp.einsum('bhld,bhdv->bhlv', q, C) + jnp.einsum('bhlm,bhmv->bhlv', qk, v)
    den = w_inter * jnp.einsum('bhld,bhd->bhl', q, n) + jnp.sum(qk, axis=-1)
    h = num / jnp.maximum(jnp.abs(den), jnp.exp(-m_t))[..., None]
    m_new = m_t[..., -1]
    w_fin = jnp.exp(b[..., -1:] - b + ig - m_new[..., None])
    dec0 = jnp.exp(b[..., -1] + m - m_new)
    C_new = dec0[..., None, None] * C + jnp.einsum('bhl,bhld,bhlv->bhdv', w_fin, k, v)
    n_new = dec0[..., None] * n + jnp.einsum('bhl,bhld->bhd', w_fin, k)
    return (C_new, n_new, m_new), jnp.swapaxes(h, 1, 2)


def run_chunks(step, state, xs, lead):
    state, out_head = step(state, tuple(a[:, :lead] for a in xs))
    rest = xs[0].shape[1] - lead
    if rest == 0:
        return state, out_head
    n_blk = rest // CHUNK
    blocks = tuple(jnp.swapaxes(a[:, lead:].reshape((a.shape[0], n_blk, CHUNK) + a.shape[2:]), 0, 1) for a in xs)
    state, out_rest = lax.scan(step, state, blocks)
    out_rest = jnp.swapaxes(out_rest, 0, 1)
    out_rest = out_rest.reshape((out_rest.shape[0], rest) + out_rest.shape[3:])
    return state, jnp.concatenate([out_head, out_rest], axis=1)


def mixer(x, lp, rel_bias, lam_init, past):
    f32 = jnp.float32
    B, L, _ = x.shape
    z = rmsnorm(x, lp['norm_mix']) @ lp['w_in']
    da_q, da_k, da_v, g_qkv, g_b, g_a, g_gate, m_qkv, m_i, m_f, m_o = split_cols(z)
    q = rmsnorm(da_q.reshape(B, L, DA_HEADS, 2, DA_DK), lp['da_q_norm'])
    k = rmsnorm(da_k.reshape(B, L, DA_HEADS, 2, DA_DK), lp['da_k_norm'])
    v = da_v.reshape(B, L, DA_HEADS, DA_DV)
    lam = (jnp.exp(jnp.sum(lp['da_lq1'].astype(f32) * lp['da_lk1'].astype(f32)))
           - jnp.exp(jnp.sum(lp['da_lq2'].astype(f32) * lp['da_lk2'].astype(f32))) + lam_init)
    if past is None:
        pos = jnp.arange(L)
        chunk = prompt_chunk_ids(pos)
        n_blk = -(-L // Q_BLOCK)
        L_pad = n_blk * Q_BLOCK
        q_pad = jnp.pad(q, ((0, 0), (0, L_pad - L), (0, 0), (0, 0), (0, 0)))
        q_blocks = jnp.swapaxes(q_pad.reshape(B, n_blk, Q_BLOCK, DA_HEADS, 2, DA_DK), 0, 1)
        pos_q = jnp.arange(L_pad)
        blocks = (q_blocks, pos_q.reshape(n_blk, Q_BLOCK), prompt_chunk_ids(pos_q).reshape(n_blk, Q_BLOCK))
        o_da = lax.map(lambda blk: diff_attend(blk[0], blk[1], blk[2], k, v, pos, chunk, rel_bias, lam), blocks)
        o_da = jnp.swapaxes(o_da, 0, 1).reshape(B, L_pad, DA_HEADS, DA_DV)[:, :L]
        conv_prev = jnp.zeros((B, GDN_CONV - 1, GDN_QKV_COLS), x.dtype)
        S0 = jnp.zeros((B, GDN_HEADS, GDN_DK, GDN_DV), f32)
        C0 = jnp.zeros((B, ML_HEADS, ML_DK, ML_DV), f32)
        n0 = jnp.zeros((B, ML_HEADS, ML_DK), f32)
        m0 = jnp.zeros((B, ML_HEADS), f32)
        lead = N_META
    else:
        cache_k, cache_v, conv_prev, S0, C0, n0, m0 = past
        P = cache_k.shape[1]
        k_all = jnp.concatenate([cache_k.astype(k.dtype), k], axis=1)
        v_all = jnp.concatenate([cache_v.astype(v.dtype), v], axis=1)
        k_pos = jnp.arange(P + L)
        k_chunk = (k_pos >= P).astype(jnp.int32)
        o_da = diff_attend(q, P + jnp.arange(L), jnp.ones((L,), jnp.int32), k_all, v_all, k_pos, k_chunk, rel_bias, lam)
        S0, C0, n0, m0 = S0.astype(f32), C0.astype(f32), n0.astype(f32), m0.astype(f32)
        lead = L
    o_da = rmsnorm(o_da, lp['da_out_norm']).astype(f32) * (1.0 - lam_init)
    gc, conv_state = causal_dwconv(g_qkv, conv_prev, lp['gdn_conv_w'])
    gc = jax.nn.silu(gc.astype(f32))
    nqk = GDN_HEADS * GDN_DK
    gq = l2norm(gc[..., :nqk].reshape(B, L, GDN_HEADS, GDN_DK)) * (GDN_DK ** -0.5)
    gk = l2norm(gc[..., nqk:2 * nqk].reshape(B, L, GDN_HEADS, GDN_DK))
    gv = gc[..., 2 * nqk:].reshape(B, L, GDN_HEADS, GDN_DV)
    beta = jax.nn.sigmoid(g_b.astype(f32))
    g_log = -jnp.exp(lp['gdn_A_log'].astype(f32)) * jax.nn.softplus(g_a.astype(f32) + lp['gdn_dt_bias'].astype(f32))
    S, o_g = run_chunks(gdn_chunk, S0, (gq, gk, gv, beta, g_log), lead)
    o_g = rmsnorm(o_g, lp['gdn_out_norm']) * jax.nn.silu(g_gate.astype(f32).reshape(B, L, GDN_HEADS, GDN_DV))
    mf = m_qkv.astype(f32)
    mqk = ML_HEADS * ML_DK
    mq = mf[..., :mqk].reshape(B, L, ML_HEADS, ML_DK)
    mk = mf[..., mqk:2 * mqk].reshape(B, L, ML_HEADS, ML_DK) * (ML_DK ** -0.5)
    mv = mf[..., 2 * mqk:].reshape(B, L, ML_HEADS, ML_DV)
    i_pre = m_i.astype(f32) + lp['ml_i_bias'].astype(f32)
    log_f = jax.nn.log_sigmoid(m_f.astype(f32) + lp['ml_f_bias'].astype(f32))
    (C, n, m), o_m = run_chunks(mlstm_chunk, (C0, n0, m0), (mq, mk, mv, i_pre, log_f), lead)
    o_m = rmsnorm(o_m, lp['ml_out_norm']) * jax.nn.sigmoid(m_o.astype(f32).reshape(B, L, ML_HEADS, ML_DV))
    mix = jnp.concatenate([o_da.reshape(B, L, DA_HEADS * DA_DV), o_g.reshape(B, L, GDN_HEADS * GDN_DV),
                           o_m.reshape(B, L, ML_HEADS * ML_DV)], axis=-1).astype(x.dtype)
    return x + mix @ lp['w_out'], (k, v, conv_state, S, C, n, m)


def conv_ffn(x, norm_w, w_up, conv_w, w_down, prev):
    up = rmsnorm(x, norm_w) @ w_up
    gate, val = up[..., :FFN_DIM], up[..., FFN_DIM:]
    gate, new_prev = causal_dwconv(gate, prev, conv_w)
    return x + (jax.nn.silu(gate) * val) @ w_down, new_prev


def setup_inputs(seed: int = 0) -> dict:
    key = jax.random.key(seed)
    ks = iter(jax.random.split(key, 48))
    f32 = jnp.float32

    def nrm(shape, scale=1.0):
        return jax.random.normal(next(ks), shape, f32) * scale

    def gain(shape):
        return 1.0 + 0.02 * jax.random.normal(next(ks), shape, f32)

    dt = jnp.exp(jax.random.uniform(next(ks), (DEPTH, GDN_HEADS), f32, math.log(1e-3), math.log(1e-1)))
    return {
        'x_prompt': nrm((BATCH, SEQ, D_MODEL)),
        'x_sample': nrm((DEC_BATCH, DEC_SEQ, D_MODEL)),
        'cache_attn_k': nrm((DEPTH, DEC_BATCH, PAST_LEN, DA_HEADS, 2, DA_DK)),
        'cache_attn_v': nrm((DEPTH, DEC_BATCH, PAST_LEN, DA_HEADS, DA_DV)),
        'state_gdn_conv': nrm((DEPTH, DEC_BATCH, GDN_CONV - 1, GDN_QKV_COLS)),
        'state_gdn_S': nrm((DEPTH, DEC_BATCH, GDN_HEADS, GDN_DK, GDN_DV), GDN_DK ** -0.5),
        'state_mlstm_C': nrm((DEPTH, DEC_BATCH, ML_HEADS, ML_DK, ML_DV), 0.1),
        'state_mlstm_n': nrm((DEPTH, DEC_BATCH, ML_HEADS, ML_DK), 0.1),
        'state_mlstm_m': nrm((DEPTH, DEC_BATCH, ML_HEADS)),
        'state_ffn_conv': nrm((DEPTH, DEC_BATCH, FFN_CONV - 1, FFN_DIM)),
        'meta_tokens': nrm((N_META, D_MODEL)),
        'rel_bias': nrm((NUM_BUCKETS, DA_HEADS), 0.5),
        'norm_mix': gain((DEPTH, D_MODEL)),
        'norm_ffn': gain((DEPTH, D_MODEL)),
        'w_in': nrm((DEPTH, D_MODEL, IN_COLS), D_MODEL ** -0.5),
        'w_out': nrm((DEPTH, D_MIX, D_MODEL), D_MIX ** -0.5),
        'da_q_norm': gain((DEPTH, DA_DK)),
        'da_k_norm': gain((DEPTH, DA_DK)),
        'da_lq1': nrm((DEPTH, DA_DK), 0.1),
        'da_lk1': nrm((DEPTH, DA_DK), 0.1),
        'da_lq2': nrm((DEPTH, DA_DK), 0.1),
        'da_lk2': nrm((DEPTH, DA_DK), 0.1),
        'da_out_norm': gain((DEPTH, DA_DV)),
        'gdn_conv_w': nrm((DEPTH, GDN_CONV, GDN_QKV_COLS), GDN_CONV ** -0.5),
        'gdn_A_log': jnp.log(jax.random.uniform(next(ks), (DEPTH, GDN_HEADS), f32, 1.0, 16.0)),
        'gdn_dt_bias': dt + jnp.log(-jnp.expm1(-dt)),
        'gdn_out_norm': gain((DEPTH, GDN_DV)),
        'ml_i_bias': nrm((DEPTH, ML_HEADS), 0.1),
        'ml_f_bias': 3.0 + 3.0 * jax.random.uniform(next(ks), (DEPTH, ML_HEADS), f32),
        'ml_out_norm': gain((DEPTH, ML_DV)),
        'ffn_w_up': nrm((DEPTH, D_MODEL, 2 * FFN_DIM), D_MODEL ** -0.5),
        'ffn_conv_w': nrm((DEPTH, FFN_CONV, FFN_DIM), FFN_CONV ** -0.5),
        'ffn_w_down': nrm((DEPTH, FFN_DIM, D_MODEL), FFN_DIM ** -0.5),
    }


def reference(x_prompt, x_sample, cache_attn_k, cache_attn_v, state_gdn_conv, state_gdn_S, state_mlstm_C,
              state_mlstm_n, state_mlstm_m, state_ffn_conv, meta_tokens, rel_bias, norm_mix, norm_ffn, w_in, w_out,
              da_q_norm, da_k_norm, da_lq1, da_lk1, da_lq2, da_lk2, da_out_norm, gdn_conv_w, gdn_A_log, gdn_dt_bias,
              gdn_out_norm, ml_i_bias, ml_f_bias, ml_out_norm, ffn_w_up, ffn_conv_w, ffn_w_down):
    B = x_prompt.shape[0]
    meta = jnp.broadcast_to(meta_tokens.astype(x_prompt.dtype)[None], (B, N_META, D_MODEL))
    xp = jnp.concatenate([meta, x_prompt], axis=1)
    xs = x_sample
    p_states = [[] for _ in range(8)]
    s_states = [[] for _ in range(8)]
    for l in range(DEPTH):
        lp = {'norm_mix': norm_mix[l], 'w_in': w_in[l], 'w_out': w_out[l], 'da_q_norm': da_q_norm[l],
              'da_k_norm': da_k_norm[l], 'da_lq1': da_lq1[l], 'da_lk1': da_lk1[l], 'da_lq2': da_lq2[l],
              'da_lk2': da_lk2[l], 'da_out_norm': da_out_norm[l], 'gdn_conv_w': gdn_conv_w[l],
              'gdn_A_log': gdn_A_log[l], 'gdn_dt_bias': gdn_dt_bias[l], 'gdn_out_norm': gdn_out_norm[l],
              'ml_i_bias': ml_i_bias[l], 'ml_f_bias': ml_f_bias[l], 'ml_out_norm': ml_out_norm[l]}
        lam_init = 0.8 - 0.6 * math.exp(-0.3 * l)
        xp, st_p = mixer(xp, lp, rel_bias, lam_init, None)
        xs, st_s = mixer(xs, lp, rel_bias, lam_init,
                         (cache_attn_k[l], cache_attn_v[l], state_gdn_conv[l], state_gdn_S[l],
                          state_mlstm_C[l], state_mlstm_n[l], state_mlstm_m[l]))
        xp, fp = conv_ffn(xp, norm_ffn[l], ffn_w_up[l], ffn_conv_w[l], ffn_w_down[l],
                          jnp.zeros((B, FFN_CONV - 1, FFN_DIM), xp.dtype))
        xs, fs = conv_ffn(xs, norm_ffn[l], ffn_w_up[l], ffn_conv_w[l], ffn_w_down[l], state_ffn_conv[l])
        for i, a in enumerate(st_p + (fp,)):
            p_states[i].append(a)
        for i, a in enumerate(st_s + (fs,)):
            s_states[i].append(a)
    k_p, v_p, gconv_p, gS_p, mC_p, mn_p, mm_p, fconv_p = [jnp.stack(s) for s in p_states]
    k_s, v_s, gconv_s, gS_s, mC_s, mn_s, mm_s, fconv_s = [jnp.stack(s) for s in s_states]
    y_prompt = xp[:, N_META:]
    return (y_prompt, xs, k_p, v_p, gconv_p, gS_p, mC_p, mn_p, mm_p, fconv_p,
            k_s, v_s, gconv_s, gS_s, mC_s, mn_s, mm_s, fconv_s)
```

```python
import math
import os
from contextlib import ExitStack

import numpy as np
import concourse.bass as bass
import concourse.mybir as mybir
from concourse.bass_utils import run_bass_kernel_spmd

F32 = mybir.dt.float32
BF16 = mybir.dt.bfloat16
AF = mybir.ActivationFunctionType
ALU = mybir.AluOpType
AX = mybir.AxisListType

D = 2048
NMETA = 16
HD = 128
NH = 4
FFN = 5632
INC = 7184
EPS = 1e-6
NEG = -30000.0
O_DQ, O_DK, O_DV, O_GQKV, O_GB, O_GA, O_GG, O_MQKV, O_MI, O_MF, O_MO = 0, 1024, 2048, 3072, 4608, 4612, 4616, 5128, 6664, 6668, 6672


class Buf:
    __slots__ = ("name", "w", "r", "sem", "semval", "excl")

    def __init__(self, name, excl=False):
        self.name = name
        self.w = None
        self.r = []
        self.sem = None
        self.semval = 0
        self.excl = excl


class Eng:
    def __init__(self, S, name, eng):
        self.name = name
        self.eng = eng
        self.sem = S.new_sem("e_" + name)
        self.count = 0
        self.idx = 0
        self.last = None
        self.ms = []
        self.waited = {}


class Sync:
    def __init__(self, nc, stack):
        self.nc = nc
        self.stack = stack
        self.E = {}
        self.sempool = {}
        for n, e in (("pe", nc.tensor), ("act", nc.scalar), ("dve", nc.vector), ("pool", nc.gpsimd), ("sp", nc.sync)):
            self.E[n] = Eng(self, n, e)
        self.ninst = 0

    def new_sem(self, name):
        return self.stack.enter_context(self.nc.semaphore(name))

    def _resolve(self, dep):
        if dep[0] == "d":
            return dep[1], dep[2]
        e = self.E[dep[1]]
        idx = dep[2]
        ms = e.ms
        if ms and ms[-1][0] >= idx:
            lo, hi = 0, len(ms) - 1
            while lo < hi:
                mid = (lo + hi) // 2
                if ms[mid][0] >= idx:
                    hi = mid
                else:
                    lo = mid + 1
            return e.sem, ms[lo][1]
        e.last.then_inc(e.sem, 1)
        e.count += 1
        e.ms.append((e.idx, e.count))
        return e.sem, e.count

    def _wait(self, en, deps):
        need = {}
        for d in deps:
            if d is None:
                continue
            if d[0] == "c" and d[1] == en.name and en.name == "pe":
                continue
            sem, val = self._resolve(d)
            k = id(sem)
            if en.waited.get(k, 0) >= val:
                continue
            if k not in need or need[k][1] < val:
                need[k] = (sem, val)
        for k, (sem, val) in need.items():
            en.eng.wait_ge(sem, val)
            en.waited[k] = val

    def _deps(self, reads, writes):
        deps = []
        for b in reads:
            deps.append(b.w)
            if b.excl:
                deps.extend(b.r)
        for b in writes:
            deps.append(b.w)
            deps.extend(b.r)
        return deps

    def _record(self, me, engname, reads, writes):
        for b in reads:
            if engname is not None:
                b.r = [x for x in b.r if not (x[0] == "c" and x[1] == engname)]
            b.r.append(me)
        for b in writes:
            b.w = me
            b.r = []

    def op(self, engname, fn, reads=(), writes=()):
        en = self.E[engname]
        self._wait(en, self._deps(reads, writes))
        inst = fn(en.eng)
        self.ninst += 1
        en.idx += 1
        en.last = inst
        self._record(("c", engname, en.idx), engname, reads, writes)
        return inst

    def dma(self, qname, out, in_, reads=(), writes=(), sembuf=None):
        en = self.E[qname]
        self._wait(en, self._deps(reads, writes))
        sb = sembuf if sembuf is not None else (writes[0] if writes else reads[0])
        if sb.sem is None:
            if sb.name not in self.sempool:
                self.sempool[sb.name] = [self.new_sem("d_" + sb.name), 0]
            sb.sem = self.sempool[sb.name][0]
        pe_ = self.sempool[sb.name]
        inst = en.eng.dma_start(out=out, in_=in_)
        inst.then_inc(sb.sem, 16)
        self.ninst += 1
        pe_[1] += 16
        sb.semval = pe_[1]
        self._record(("d", sb.sem, sb.semval), None, reads, writes)
        return inst

    def barrier(self):
        deps = []
        for n, e in self.E.items():
            if e.last is not None:
                deps.append(("c", n, e.idx))
        for nm, (sem_, val_) in self.sempool.items():
            if val_:
                deps.append(("d", sem_, val_))
        resolved = [self._resolve(d) for d in deps]
        for n, en in self.E.items():
            for sem, val in resolved:
                if sem is en.sem:
                    continue
                k = id(sem)
                if en.waited.get(k, 0) >= val:
                    continue
                en.eng.wait_ge(sem, val)
                en.waited[k] = val


class T:
    scope = None
    nalloc = 0

    def __init__(self, alloc, name, excl=False):
        self._alloc = alloc
        self.name = name
        self.excl = excl
        self._t = None
        self._b = None

    def _ensure(self):
        if self._t is None:
            T.nalloc += 1
            self._t = T.scope[0].enter_context(self._alloc("%s_%d" % (self.name, T.nalloc)))
            self._b = Buf(self.name, self.excl)
            T.scope[1].append(self)

    @property
    def t(self):
        self._ensure()
        return self._t

    @property
    def b(self):
        self._ensure()
        return self._b

    def __getitem__(self, k):
        return self.t[k]


def t5_bucket_np(rel):
    rel = np.asarray(rel, np.int64)
    half, exact = 16, 8
    n = np.abs(rel)
    nf = np.maximum(n, 1).astype(np.float32)
    large = exact + (np.log(nf / np.float32(exact)) / np.float32(math.log(128 / exact)) * np.float32(half - exact)).astype(np.int32)
    large = np.minimum(large, half - 1)
    return np.where(rel > 0, half, 0) + np.where(n < exact, n, large)


def bucket_ranges(lo, hi):
    rels = np.arange(lo, hi + 1)
    b = t5_bucket_np(rels)
    out = []
    s = 0
    for i in range(1, len(rels) + 1):
        if i == len(rels) or b[i] != b[s]:
            out.append((int(b[s]), int(rels[s]), int(rels[i - 1])))
            s = i
    return out


def build(NT, PAST):
    NP = NMETA + 128 * NT
    NS = 64
    NKS = PAST + NS
    nc = bass.Bass("TRN2", target_bir_lowering=False)
    st = ExitStack()
    st.enter_context(nc.allow_non_contiguous_dma(reason="few tiny strided state vectors"))
    S = Sync(nc, st)

    def din(name, shape, dt=F32):
        return nc.dram_tensor(name, list(shape), dt, kind="ExternalInput").ap()

    def dout(name, shape):
        return nc.dram_tensor(name, list(shape), F32, kind="ExternalOutput").ap()

    def dscr(name, shape, dt):
        return nc.dram_tensor(name, list(shape), dt, kind="Internal").ap()

    T.scope = [st, []]

    def sb(name, shape, dt=F32):
        return T(lambda nm: nc.sbuf_tensor(nm, list(shape), dt), name)

    def ps(name, shape, dt=F32):
        return T(lambda nm: nc.psum_tensor(nm, list(shape), dt), name, excl=True)

    def scoped(fn):
        S.barrier()
        with ExitStack() as s2:
            old = T.scope
            T.scope = [s2, []]
            fn()
            S.barrier()
            for t_ in T.scope[1]:
                t_._t = None
                t_._b = None
            T.scope = old

    def bl(ts):
        return [x if isinstance(x, Buf) else x.b for x in ts]

    def op(eng, fn, R=(), W=()):
        return S.op(eng, fn, bl(R), bl(W))

    def dma(q, out, in_, R=(), W=(), sem=None):
        return S.dma(q, out, in_, bl(R), bl(W), sembuf=(sem.b if isinstance(sem, T) else sem))

    def mm(out, lhsT, rhs, start=True, stop=True, R=(), W=()):
        return op("pe", lambda e: e.matmul(out, lhsT=lhsT, rhs=rhs, start=start, stop=stop), R, W)

    def rsq(ap, tl, add_eps=False):
        if add_eps:
            op("dve", lambda e: e.tensor_scalar_add(ap, ap, EPS), R=[tl], W=[tl])
        op("act", lambda e: e.activation(out=ap, in_=ap, func=AF.Ln), R=[tl], W=[tl])
        op("act", lambda e: e.activation(out=ap, in_=ap, func=AF.Exp, scale=-0.5), R=[tl], W=[tl])

    def tr(out, in_, ident, R=(), W=()):
        return op("pe", lambda e: e.transpose(out, in_, ident), R, W)

    xp = din("xp", [NP, D])
    xs = din("xs", [NS, D])
    ck = din("ck", [2, PAST, 1024])
    cv = din("cv", [2, PAST, 1024])
    gconv_in = din("gconv", [2, 3, 1536])
    gS_in = din("gS", [2, NH, 128, 128])
    mC_in = din("mC", [2, NH, 128, 128])
    mn_in = din("mn", [2, NH, 128])
    mm_in = din("mm", [2, NH])
    fconv_in = din("fconv", [2, 2, FFN])
    rel_bias = din("rel_bias", [32, NH])
    norm_mix = din("norm_mix", [2, D])
    norm_ffn = din("norm_ffn", [2, D])
    w_in = din("w_in", [2, D, INC])
    w_out = din("w_out", [2, D, D])
    da_q_norm = din("da_q_norm", [2, 128])
    da_k_norm = din("da_k_norm", [2, 128])
    da_l = din("da_l", [2, 4, 128])
    da_out_norm = din("da_out_norm", [2, 256])
    gdn_conv_w = din("gdn_conv_w", [2, 4, 1536])
    gdn_A_log = din("gdn_A_log", [2, NH])
    gdn_dt_bias = din("gdn_dt_bias", [2, NH])
    gdn_out_norm = din("gdn_out_norm", [2, 128])
    ml_i_bias = din("ml_i_bias", [2, NH])
    ml_f_bias = din("ml_f_bias", [2, NH])
    ml_out_norm = din("ml_out_norm", [2, 128])
    w_up = din("ffn_w_up", [2, D, 2 * FFN])
    ffn_conv_w = din("ffn_conv_w", [2, 3, FFN])
    w_down = din("ffn_w_down", [2, FFN, D])

    y_p = dout("y_p", [NP - NMETA, D])
    y_s = dout("y_s", [NS, D])
    k_p = dout("k_p", [2, NP, 1024])
    v_p = dout("v_p", [2, NP, 1024])
    gc_p = dout("gc_p", [2, 3, 1536])
    gS_p = dout("gS_p", [2, NH, 128, 128])
    mC_p = dout("mC_p", [2, NH, 128, 128])
    mn_p = dout("mn_p", [2, NH, 128])
    mm_p = dout("mm_p", [2, NH])
    fc_p = dout("fc_p", [2, 2, FFN])
    k_s = dout("k_s", [2, NS, 1024])
    v_s = dout("v_s", [2, NS, 1024])
    gc_s = dout("gc_s", [2, 3, 1536])
    gS_s = dout("gS_s", [2, NH, 128, 128])
    mC_s = dout("mC_s", [2, NH, 128, 128])
    mn_s = dout("mn_s", [2, NH, 128])
    mm_s = dout("mm_s", [2, NH])
    fc_s = dout("fc_s", [2, 2, FFN])
    OUT = Buf("outputs")

    NCB_IN = (INC + 511) // 512
    Wb_in = dscr("Wb_in", [NCB_IN, 128, 16, 512], BF16)
    Wb_out = dscr("Wb_out", [4, 128, 16, 512], BF16)
    Wb_up = dscr("Wb_up", [88, 128, 16, 128], BF16)
    Wb_down = dscr("Wb_down", [4, 128, 44, 512], BF16)
    SCR = Buf("scratch")

    class Seq:
        pass

    def mkseq(name, n, nkeys, lead_tiles):
        q = Seq()
        q.name = name
        q.n = n
        q.nk = nkeys
        q.koff = nkeys - n
        q.tiles = lead_tiles
        q.Zp = [dscr("Z0_" + name, [3 + n, 3072], F32), dscr("Z1_" + name, [3 + n, 2048], F32), dscr("Z2_" + name, [3 + n, INC - 5120], F32)]
        q.QT = dscr("QT_" + name, [8, 128, n], BF16)
        q.KT = dscr("KT_" + name, [8, 128, nkeys], BF16)
        q.V = dscr("V_" + name, [nkeys, NH, 256], BF16)
        q.MIXT = dscr("MIXT_" + name, [D, n], BF16)
        q.X1 = dscr("X1_" + name, [n, D], F32)
        return q

    ZOFF = [0, 3072, 5120, INC]

    def zc(seq, rows, c0, c1):
        for i in range(3):
            if ZOFF[i] <= c0 and c1 <= ZOFF[i + 1]:
                return seq.Zp[i][rows, c0 - ZOFF[i]:c1 - ZOFF[i]]
        raise AssertionError((c0, c1))

    ptiles = [(0, NMETA)] + [(NMETA + 128 * i, 128) for i in range(NT)]
    P = mkseq("p", NP, NP, ptiles)
    Q = mkseq("s", NS, NKS, [(0, NS)])
    P.x0, Q.x0 = xp, xs
    P.prompt, Q.prompt = True, False

    ident_f = sb("ident_f", [128, 128])
    ident_b = sb("ident_b", [128, 128], BF16)
    ones_f = sb("ones_f", [128, 128])
    ones_b = sb("ones_b", [128, 128], BF16)
    U_le = sb("U_le", [128, 128])
    U_gt = sb("U_gt", [128, 128])
    negm_T = sb("negm_T", [128, 128])
    negm = sb("negm", [128, 128])
    m_strictT = sb("m_strictT", [128, 128])
    zero_f = sb("zero_f", [128, 512])
    op("pool", lambda e: e.memset(zero_f[:], 0.0), W=[zero_f])
    op("pool", lambda e: e.memset(ones_f[:], 1.0), W=[ones_f])
    op("pool", lambda e: e.memset(ones_b[:], 1.0), W=[ones_b])
    op("pool", lambda e: e.memset(ident_f[:], 0.0), W=[ident_f])
    op("pool", lambda e: e.affine_select(out=ident_f[:], in_=ident_f[:], pattern=[[-1, 128]], compare_op=ALU.not_equal, fill=1.0, base=0, channel_multiplier=1), R=[ident_f], W=[ident_f])
    op("dve", lambda e: e.tensor_copy(ident_b[:], ident_f[:]), R=[ident_f], W=[ident_b])
    op("pool", lambda e: e.affine_select(out=U_le[:], in_=ones_f[:], pattern=[[1, 128]], compare_op=ALU.is_ge, fill=0.0, base=0, channel_multiplier=-1), R=[ones_f], W=[U_le])
    op("pool", lambda e: e.affine_select(out=U_gt[:], in_=ones_f[:], pattern=[[-1, 128]], compare_op=ALU.is_gt, fill=0.0, base=0, channel_multiplier=1), R=[ones_f], W=[U_gt])
    op("pool", lambda e: e.affine_select(out=negm_T[:], in_=zero_f[:, 0:128], pattern=[[1, 128]], compare_op=ALU.is_ge, fill=NEG, base=0, channel_multiplier=-1), R=[zero_f], W=[negm_T])
    op("pool", lambda e: e.affine_select(out=negm[:], in_=zero_f[:, 0:128], pattern=[[-1, 128]], compare_op=ALU.is_ge, fill=NEG, base=0, channel_multiplier=1), R=[zero_f], W=[negm])
    op("pool", lambda e: e.affine_select(out=m_strictT[:], in_=ones_f[:], pattern=[[1, 128]], compare_op=ALU.is_gt, fill=0.0, base=0, channel_multiplier=-1), R=[ones_f], W=[m_strictT])

    SEL = {}
    for n_ in (NMETA, 64, 128):
        SEL[n_] = sb("sel%d" % n_, [128, 128])
        op("pool", lambda e: e.memset(SEL[n_][:], 0.0), W=[SEL[n_]])
        op("pool", lambda e: e.affine_select(out=SEL[n_][:], in_=SEL[n_][:], pattern=[[0, 128]], compare_op=ALU.not_equal, fill=1.0, base=-(n_ - 1), channel_multiplier=1), R=[SEL[n_]], W=[SEL[n_]])

    PB = [ps("pb%d" % i, [128, 512]) for i in range(8)]

    for p_ in PB:
        p_._ensure()

    wst = [sb("wst%d" % i, [128, 8192], BF16) for i in range(2)]
    wctr = [0]

    wso = [Buf("wso0"), Buf("wso1")]

    def stage():
        wctr[0] += 1
        return wst[wctr[0] % 2]

    def sso():
        return wso[wctr[0] % 2]

    def precast(l):
        wv = w_in[l].rearrange("(c p) n -> p c n", p=128)
        for cb in range(NCB_IN):
            c0 = cb * 512
            nco = min(512, INC - c0)
            s_ = stage()
            sv = s_.t[:].rearrange("p (c n) -> p c n", n=512)
            dma("pool", sv[:, :, 0:nco], wv[:, :, c0:c0 + nco], W=[s_])
            dma("sp", Wb_in[cb][:, :, 0:nco], sv[:, :, 0:nco], R=[s_], W=[SCR], sem=sso())
        wv = w_out[l].rearrange("(c p) n -> p c n", p=128)
        for cb in range(4):
            s_ = stage()
            sv = s_.t[:].rearrange("p (c n) -> p c n", n=512)
            dma("pool", sv, wv[:, :, cb * 512:(cb + 1) * 512], W=[s_])
            dma("sp", Wb_out[cb], sv, R=[s_], W=[SCR], sem=sso())
        wv = w_up[l].rearrange("(c p) n -> p c n", p=128)
        for j4 in range(22):
            s_ = stage()
            sv = s_.t[:].rearrange("p (c n) -> p c n", n=512)
            dma("pool", sv, wv[:, :, j4 * 512:(j4 + 1) * 512], W=[s_])
            for jj in range(4):
                dma("sp", Wb_up[j4 * 4 + jj], sv[:, :, jj * 128:(jj + 1) * 128], R=[s_], W=[SCR], sem=sso())
        wv = w_down[l].rearrange("(c p) n -> p c n", p=128)
        for cb in range(4):
            for half in range(4):
                s_ = stage()
                sv = s_.t[:, 0:11 * 512].rearrange("p (c n) -> p c n", n=512)
                dma("pool", sv, wv[:, half * 11:(half + 1) * 11, cb * 512:(cb + 1) * 512], W=[s_])
                dma("sp", Wb_down[cb][:, half * 11:(half + 1) * 11, :], sv, R=[s_], W=[SCR], sem=sso())

    gmix_bc = sb("gmix_bc", [128, D])
    gffn_bc = sb("gffn_bc", [128, D])
    qn_bc = sb("qn_bc", [128, 128])
    kn_bc = sb("kn_bc", [128, 128])
    don_bc = sb("don_bc", [128, 256])
    gon_bc = sb("gon_bc", [128, 128])
    mon_bc = sb("mon_bc", [128, 128])
    gcw_bc = sb("gcw_bc", [128, 4, 1536])
    small_bc = sb("small_bc", [128, 16])
    negA_bc = sb("negA_bc", [128, 4])
    dal = sb("dal", [1, 4, 128])
    lam_s = sb("lam_s", [1, 8])
    nlam_bc = sb("nlam_bc", [128, 1])
    rb_bc = sb("rb_bc", [128, 32, NH])
    rbd_bc = sb("rbd_bc", [128, 32, NH])
    fcw = sb("fcw", [128, 44, 3])

    rowst = sb("rowst", [4, 1408])

    def rows_to_fm(src_ap, nr, dst):
        for q4 in range(4):
            dma("sp", rowst[0:nr, :], src_ap[:, q4 * 1408:(q4 + 1) * 1408], W=[rowst])
            pb = PB[7]
            for c in range(11):
                tr(pb[:, c * nr:(c + 1) * nr], rowst[0:nr, c * 128:(c + 1) * 128], ident_f[0:nr, 0:nr], R=[rowst, ident_f], W=[pb])
            op("dve", lambda e: e.tensor_copy(dst[:, q4 * 11:(q4 + 1) * 11, :], pb[:, 0:11 * nr].rearrange("p (c t) -> p c t", t=nr)), R=[pb], W=[dst])

    def fm_to_rows(src, nr, dst_ap, W):
        for q4 in range(4):
            for g in range(3):
                pb = PB[4 + (g % 2)]
                ncc = 4 if g < 2 else 3
                for cc in range(ncc):
                    c = q4 * 11 + g * 4 + cc
                    tr(pb[0:nr, cc * 128:(cc + 1) * 128], src[:, c, :], ident_f[:, :], R=[src, ident_f], W=[pb])
                op("dve", lambda e: e.tensor_copy(rowst[0:nr, g * 512:g * 512 + ncc * 128], pb[0:nr, 0:ncc * 128]), R=[pb], W=[rowst])
            dma("sp", dst_ap[:, q4 * 1408:(q4 + 1) * 1408], rowst[0:nr, :], R=[rowst], W=W, sem=rowst)

    def load_params(l, lam_init):
        dma("sp", gmix_bc[:], norm_mix[l:l + 1, :].broadcast_to([128, D]), W=[gmix_bc])
        dma("sp", gffn_bc[:], norm_ffn[l:l + 1, :].broadcast_to([128, D]), W=[gffn_bc])
        dma("sp", qn_bc[:], da_q_norm[l:l + 1, :].broadcast_to([128, 128]), W=[qn_bc])
        dma("sp", kn_bc[:], da_k_norm[l:l + 1, :].broadcast_to([128, 128]), W=[kn_bc])
        dma("sp", don_bc[:], da_out_norm[l:l + 1, :].broadcast_to([128, 256]), W=[don_bc])
        dma("sp", gon_bc[:], gdn_out_norm[l:l + 1, :].broadcast_to([128, 128]), W=[gon_bc])
        dma("sp", mon_bc[:], ml_out_norm[l:l + 1, :].broadcast_to([128, 128]), W=[mon_bc])
        dma("sp", small_bc[:, 0:4], gdn_A_log[l:l + 1, :].broadcast_to([128, 4]), W=[small_bc])
        dma("sp", small_bc[:, 4:8], gdn_dt_bias[l:l + 1, :].broadcast_to([128, 4]), W=[small_bc])
        dma("sp", small_bc[:, 8:12], ml_i_bias[l:l + 1, :].broadcast_to([128, 4]), W=[small_bc])
        dma("sp", small_bc[:, 12:16], ml_f_bias[l:l + 1, :].broadcast_to([128, 4]), W=[small_bc])
        dma("sp", dal[:], da_l[l:l + 1, :, :], W=[dal])
        rows_to_fm(ffn_conv_w[l], 3, fcw)
        op("dve", lambda e: e.tensor_scalar_mul(don_bc[:], don_bc[:], 1.0 - lam_init), R=[don_bc], W=[don_bc])
        op("act", lambda e: e.activation(out=negA_bc[:], in_=small_bc[:, 0:4], func=AF.Exp), R=[small_bc], W=[negA_bc])
        op("dve", lambda e: e.tensor_scalar_mul(negA_bc[:], negA_bc[:], -1.0), R=[negA_bc], W=[negA_bc])
        op("dve", lambda e: e.tensor_tensor(out=dal[:, 0, :], in0=dal[:, 0, :], in1=dal[:, 1, :], op=ALU.mult), R=[dal], W=[dal])
        op("dve", lambda e: e.tensor_tensor(out=dal[:, 2, :], in0=dal[:, 2, :], in1=dal[:, 3, :], op=ALU.mult), R=[dal], W=[dal])
        op("dve", lambda e: e.tensor_reduce(out=lam_s[:, 0:1], in_=dal[:, 0, :], axis=AX.X, op=ALU.add), R=[dal], W=[lam_s])
        op("dve", lambda e: e.tensor_reduce(out=lam_s[:, 1:2], in_=dal[:, 2, :], axis=AX.X, op=ALU.add), R=[dal], W=[lam_s])
        op("act", lambda e: e.activation(out=lam_s[:, 2:4], in_=lam_s[:, 0:2], func=AF.Exp), R=[lam_s], W=[lam_s])
        op("dve", lambda e: e.tensor_tensor(out=lam_s[:, 4:5], in0=lam_s[:, 3:4], in1=lam_s[:, 2:3], op=ALU.subtract), R=[lam_s], W=[lam_s])
        op("dve", lambda e: e.tensor_scalar_add(lam_s[:, 4:5], lam_s[:, 4:5], -lam_init), R=[lam_s], W=[lam_s])
        mm(PB[7][:, 0:1], ones_f[0:1, :], lam_s[0:1, 4:5], R=[ones_f, lam_s], W=[PB[7]])
        op("dve", lambda e: e.tensor_copy(nlam_bc[:], PB[7][:, 0:1]), R=[PB[7]], W=[nlam_bc])

    relt = sb("relt", [128, 512])
    mk1 = sb("mk1", [128, 512])
    mk2 = sb("mk2", [128, 512])
    bacc = [sb("bacc%d" % h, [128, 512]) for h in range(NH)]
    W_pr = [sb("W_pr%d" % h, [128, 1024], BF16) for h in range(NH)]
    W_m0 = [sb("W_m0%d" % h, [NMETA, 512], BF16) for h in range(NH)]
    W_mq = [sb("W_mq%d" % h, [NMETA, NMETA], BF16) for h in range(NH)]
    W_sp = [sb("W_sp%d" % h, [128, NS], BF16) for h in range(NH)]
    W_sd = [sb("W_sd%d" % h, [NS, NS], BF16) for h in range(NH)]

    def build_bias(dst_fn, nk, nq, base):
        op("pool", lambda e: e.iota(relt[:nk, :nq], pattern=[[-1, nq]], base=base, channel_multiplier=1, allow_small_or_imprecise_dtypes=True), W=[relt])
        for h in range(NH):
            op("pool", lambda e: e.memset(bacc[h][:nk, :nq], 0.0), W=[bacc[h]])
        lo, hi = base - (nq - 1), base + nk - 1
        for (b, r0, r1) in bucket_ranges(lo, hi):
            if b == 15:
                continue
            op("dve", lambda e: e.tensor_scalar(out=mk1[:nk, :nq], in0=relt[:nk, :nq], scalar1=float(r0) - 0.5, scalar2=None, op0=ALU.is_ge), R=[relt], W=[mk1])
            op("dve", lambda e: e.tensor_scalar(out=mk2[:nk, :nq], in0=relt[:nk, :nq], scalar1=float(r1) + 0.5, scalar2=None, op0=ALU.is_le), R=[relt], W=[mk2])
            op("dve", lambda e: e.tensor_tensor(out=mk1[:nk, :nq], in0=mk1[:nk, :nq], in1=mk2[:nk, :nq], op=ALU.mult), R=[mk1, mk2], W=[mk1])
            for h in range(NH):
                eng = "dve"
                op(eng, lambda e: e.scalar_tensor_tensor(out=bacc[h][:nk, :nq], in0=mk1[:nk, :nq], scalar=rbd_bc[:nk, b, h:h + 1], in1=bacc[h][:nk, :nq], op0=ALU.mult, op1=ALU.add), R=[mk1, rbd_bc, bacc[h]], W=[bacc[h]])
        for h in range(NH):
            dst_fn(h)

    def setup_bias():
        dma("sp", rb_bc[:].rearrange("p a b -> p (a b)"), rel_bias.rearrange("a b -> (a b)").partition_broadcast(128), W=[rb_bc])
        for h in range(NH):
            op("dve", lambda e: e.tensor_scalar(out=rbd_bc[:, :, h], in0=rb_bc[:, :, h], scalar1=rb_bc[:, 15, h:h + 1], scalar2=None, op0=ALU.subtract), R=[rb_bc], W=[rbd_bc])
        def d1(h):
            op("pool", lambda e: e.memset(W_pr[h][:], 0.0), W=[W_pr[h]])
            op("pool", lambda e: e.memset(bacc[h][64:128, 0:64], NEG), R=[bacc[h]], W=[bacc[h]])
            op("act", lambda e: e.activation(out=W_pr[h][:, 0:256], in_=bacc[h][:, 0:256], func=AF.Copy, scale=float(HD ** 0.5)), R=[bacc[h]], W=[W_pr[h]])
        build_bias(d1, 128, 256, 0)
        def d2(h):
            op("pool", lambda e: e.memset(W_m0[h][:], 0.0), W=[W_m0[h]])
            op("act", lambda e: e.activation(out=W_m0[h][:, 0:128], in_=bacc[h][:NMETA, 0:128], func=AF.Copy, scale=float(HD ** 0.5)), R=[bacc[h]], W=[W_m0[h]])
        build_bias(d2, NMETA, 128, -NMETA)
        def d3(h):
            op("act", lambda e: e.activation(out=W_mq[h][:], in_=bacc[h][:NMETA, 0:NMETA], func=AF.Copy, scale=float(HD ** 0.5)), R=[bacc[h]], W=[W_mq[h]])
        build_bias(d3, NMETA, NMETA, 0)
        def d4(h):
            op("act", lambda e: e.activation(out=W_sp[h][:], in_=bacc[h][:, 0:NS], func=AF.Copy, scale=float(HD ** 0.5)), R=[bacc[h]], W=[W_sp[h]])
        build_bias(d4, 128, NS, -128)
        def d5(h):
            op("act", lambda e: e.activation(out=W_sd[h][:], in_=bacc[h][:NS, 0:NS], func=AF.Copy, scale=float(HD ** 0.5)), R=[bacc[h]], W=[W_sd[h]])
        build_bias(d5, NS, NS, 0)

    xin = [sb("xin%d" % i, [128, D]) for i in range(2)]
    xsq = sb("xsq", [128, D], BF16)
    xnb = sb("xnb", [128, D], BF16)
    xT = sb("xT", [128, 16, 512], BF16)
    stat = sb("stat", [128, 8])
    w1 = [sb("w1_%d" % i, [128, 16, 512], BF16) for i in range(2)]
    zev = [sb("zev%d" % i, [128, 512]) for i in range(4)]
    cnt = {"x": 0, "w1": 0, "z": 0, "pb": 0}

    def rms_to_T(src_tile, n, gain_bc, dstT, col0, pbank):
        op("act", lambda e: e.activation(out=xsq[:n, :], in_=src_tile[:n, :], func=AF.Square, accum_out=stat[:n, 0:1]), R=[src_tile], W=[xsq, stat])
        op("dve", lambda e: e.tensor_scalar(out=stat[:n, 1:2], in0=stat[:n, 0:1], scalar1=1.0 / D, scalar2=EPS, op0=ALU.mult, op1=ALU.add), R=[stat], W=[stat])
        rsq(stat[:n, 1:2], stat)
        op("dve", lambda e: e.scalar_tensor_tensor(out=xnb[:n, :], in0=src_tile[:n, :], scalar=stat[:n, 1:2], in1=gain_bc[:n, :], op0=ALU.mult, op1=ALU.mult), R=[src_tile, stat, gain_bc], W=[xnb])
        for c4 in range(4):
            pb = PB[pbank + (c4 % 2)]
            pv = pb.t[:].bitcast(BF16)
            for cc in range(4):
                c = c4 * 4 + cc
                tr(pv[:, cc * 128:cc * 128 + n], xnb[:n, c * 128:(c + 1) * 128], ident_b[:n, :n], R=[xnb, ident_b], W=[pb])
            eng = "dve" if c4 % 2 == 0 else "act"
            if eng == "dve":
                op("dve", lambda e: e.tensor_copy(dstT[:, c4 * 4:(c4 + 1) * 4, col0:col0 + n], pv[:, 0:512].rearrange("p (c t) -> p c t", t=128)[:, :, 0:n]), R=[pb], W=[dstT])
            else:
                op("act", lambda e: e.activation(out=dstT[:, c4 * 4:(c4 + 1) * 4, col0:col0 + n], in_=pv[:, 0:512].rearrange("p (c t) -> p c t", t=128)[:, :, 0:n], func=AF.Copy), R=[pb], W=[dstT])

    def groups(seq):
        out = []
        tl = seq.tiles
        if seq.prompt:
            out.append([tl[0]])
            rest = tl[1:]
            for i in range(0, len(rest), 4):
                out.append(rest[i:i + 4])
        else:
            out.append(tl)
        return out

    def pass1(seq, l, xsrc):
        for grp in groups(seq):
            for ti, (r0, n) in enumerate(grp):
                xt_ = xin[cnt["x"] % 2]
                cnt["x"] += 1
                dma("sp", xt_[:n, :], xsrc[r0:r0 + n, :], R=[SCR], W=[xt_])
                rms_to_T(xt_, n, gmix_bc, xT, ti * 128, 4)
            for cb in range(NCB_IN):
                nco = min(512, INC - cb * 512)
                wt = w1[cnt["w1"] % 2]
                cnt["w1"] += 1
                dma("sp", wt[:, :, 0:nco], Wb_in[cb][:, :, 0:nco], R=[SCR], W=[wt])
                for ti, (r0, n) in enumerate(grp):
                    pb = PB[ti]
                    for c in range(16):
                        mm(pb[:n, 0:nco], xT[:, c, ti * 128:ti * 128 + n], wt[:, c, 0:nco], start=(c == 0), stop=(c == 15), R=[xT, wt], W=[pb])
                    ze = zev[cnt["z"] % 4]
                    cnt["z"] += 1
                    if cnt["z"] % 2:
                        op("dve", lambda e: e.tensor_copy(ze[:n, 0:nco], pb[:n, 0:nco]), R=[pb], W=[ze])
                    else:
                        op("act", lambda e: e.activation(out=ze[:n, 0:nco], in_=pb[:n, 0:nco], func=AF.Copy), R=[pb], W=[ze])
                    dma("sp", zc(seq, slice(3 + r0, 3 + r0 + n), cb * 512, cb * 512 + nco), ze[:n, 0:nco], R=[ze], W=[SCR], sem=ze)

    zda = [sb("zda%d" % i, [128, 3072]) for i in range(2)]
    zsq = sb("zsq", [128, 2048])
    ss16 = sb("ss16", [128, 16])
    qkn = sb("qkn", [128, 2048], BF16)
    kf = sb("kf", [128, 1024])
    vb = sb("vb", [128, 1024], BF16)
    qkT = sb("qkT", [128, 16, 128], BF16)

    def pass2(seq, l, kout, vout):
        for (r0, n) in seq.tiles:
            zt = zda[cnt["x"] % 2]
            cnt["x"] += 1
            dma("sp", zt[:n, :], zc(seq, slice(3 + r0, 3 + r0 + n), 0, 3072), R=[SCR], W=[zt])
            dma("sp", vout[l, r0:r0 + n, :], zt[:n, 2048:3072], R=[zt], W=[OUT], sem=zt)
            op("pool", lambda e: e.tensor_copy(vb[:n, :], zt[:n, 2048:3072]), R=[zt], W=[vb])
            dma("sp", seq.V[seq.koff + r0:seq.koff + r0 + n, :, :].rearrange("t h d -> t (h d)"), vb[:n, :], R=[vb], W=[SCR], sem=vb)
            op("act", lambda e: e.activation(out=zsq[:n, :], in_=zt[:n, 0:2048], func=AF.Square), R=[zt], W=[zsq])
            op("dve", lambda e: e.tensor_reduce(out=ss16[:n, :], in_=zsq[:n, :].rearrange("p (b d) -> p b d", d=128), axis=AX.X, op=ALU.add), R=[zsq], W=[ss16])
            op("dve", lambda e: e.tensor_scalar(out=ss16[:n, :], in0=ss16[:n, :], scalar1=1.0 / 128, scalar2=EPS, op0=ALU.mult, op1=ALU.add), R=[ss16], W=[ss16])
            rsq(ss16[:n, :], ss16)
            op("dve", lambda e: e.tensor_tensor(out=zt[:n, 0:2048].rearrange("p (b d) -> p b d", d=128), in0=zt[:n, 0:2048].rearrange("p (b d) -> p b d", d=128), in1=ss16[:n, :, None].broadcast_to([n, 16, 128]), op=ALU.mult), R=[zt, ss16], W=[zt])
            op("pool", lambda e: e.tensor_tensor(out=qkn[:n, 0:1024].rearrange("p (b d) -> p b d", d=128), in0=zt[:n, 0:1024].rearrange("p (b d) -> p b d", d=128), in1=qn_bc[:n, None, :].broadcast_to([n, 8, 128]), op=ALU.mult), R=[zt, qn_bc], W=[qkn])
            op("dve", lambda e: e.tensor_tensor(out=kf[:n, :].rearrange("p (b d) -> p b d", d=128), in0=zt[:n, 1024:2048].rearrange("p (b d) -> p b d", d=128), in1=kn_bc[:n, None, :].broadcast_to([n, 8, 128]), op=ALU.mult), R=[zt, kn_bc], W=[kf])
            dma("sp", kout[l, r0:r0 + n, :], kf[:n, :], R=[kf], W=[OUT], sem=kf)
            op("act", lambda e: e.activation(out=qkn[:n, 1024:2048], in_=kf[:n, :], func=AF.Copy), R=[kf], W=[qkn])
            for b4 in range(4):
                pb = PB[4 + (b4 % 2)]
                pv = pb.t[:].bitcast(BF16)
                for bb in range(4):
                    b = b4 * 4 + bb
                    tr(pv[:, bb * 128:bb * 128 + n], qkn[:n, b * 128:(b + 1) * 128], ident_b[:n, :n], R=[qkn, ident_b], W=[pb])
                if b4 % 2 == 0:
                    op("dve", lambda e: e.tensor_copy(qkT[:, b4 * 4:(b4 + 1) * 4, 0:n], pv[:, 0:512].rearrange("p (c t) -> p c t", t=128)[:, :, 0:n]), R=[pb], W=[qkT])
                else:
                    op("act", lambda e: e.activation(out=qkT[:, b4 * 4:(b4 + 1) * 4, 0:n], in_=pv[:, 0:512].rearrange("p (c t) -> p c t", t=128)[:, :, 0:n], func=AF.Copy), R=[pb], W=[qkT])
            dma("sp", seq.QT[:, :, r0:r0 + n].rearrange("b d t -> d b t"), qkT[:, 0:8, 0:n], R=[qkT], W=[SCR], sem=qkT)
            dma("sp", seq.KT[:, :, seq.koff + r0:seq.koff + r0 + n].rearrange("b d t -> d b t"), qkT[:, 8:16, 0:n], R=[qkT], W=[SCR], sem=qkT)

    def cache_prep(l):
        for i in range(PAST // 128):
            zt = zda[cnt["x"] % 2]
            cnt["x"] += 1
            dma("sp", zt[:, 0:1024], ck[l, i * 128:(i + 1) * 128, :], W=[zt])
            dma("sp", zt[:, 1024:2048], cv[l, i * 128:(i + 1) * 128, :], W=[zt])
            op("pool", lambda e: e.tensor_copy(vb[:, :], zt[:, 1024:2048]), R=[zt], W=[vb])
            dma("sp", Q.V[i * 128:(i + 1) * 128, :, :].rearrange("t h d -> t (h d)"), vb[:, :], R=[vb], W=[SCR], sem=vb)
            op("act", lambda e: e.activation(out=qkn[:, 0:1024], in_=zt[:, 0:1024], func=AF.Copy), R=[zt], W=[qkn])
            for b4 in range(2):
                pb = PB[4 + (b4 % 2)]
                pv = pb.t[:].bitcast(BF16)
                for bb in range(4):
                    b = b4 * 4 + bb
                    tr(pv[:, bb * 128:(bb + 1) * 128], qkn[:, b * 128:(b + 1) * 128], ident_b[:], R=[qkn, ident_b], W=[pb])
                op("dve", lambda e: e.tensor_copy(qkT[:, b4 * 4:(b4 + 1) * 4, :], pv[:, 0:512].rearrange("p (c t) -> p c t", t=128)), R=[pb], W=[qkT])
            dma("sp", Q.KT[:, :, i * 128:(i + 1) * 128].rearrange("b d t -> d b t"), qkT[:, 0:8, :], R=[qkT], W=[SCR], sem=qkT)

    KCH = 2048
    qT = sb("qT", [128, 8, 512], BF16)
    kT = [sb("kT%d" % i, [128, KCH], BF16) for i in range(2)]
    vt = [sb("vt%d" % i, [128, KCH // 128, 257], BF16) for i in range(2)]
    pT = [sb("pT%d" % i, [128, 512], BF16) for i in range(3)]
    o0 = sb("o0", [128, 4, 256])
    dd = sb("dd", [128, 256])
    dsq = sb("dsq", [128, 256])
    yb = sb("yb", [128, 256], BF16)
    rr = sb("rr", [128, 16])
    mixst = sb("mixst", [128, 2, 512], BF16)

    def attend(seq, l):
        for v_ in vt:
            op("pool", lambda e: e.memset(v_[:, :, 256:257], 1.0), W=[v_])
        for grp in groups(seq):
            g0 = grp[0][0]
            gn = sum(n for _, n in grp)
            is_meta = seq.prompt and grp[0][1] == NMETA
            dma("sp", qT[:, :, 0:gn], seq.QT[:, :, g0:g0 + gn].rearrange("b d t -> d b t"), R=[SCR], W=[qT])
            nkeys = (g0 + gn) if seq.prompt else seq.nk
            if is_meta:
                nkeys = NMETA
            if seq.prompt:
                ktl = [(0, NMETA)] + [(NMETA + 128 * i, 128) for i in range((nkeys - NMETA) // 128)]
            else:
                ktl = [(128 * i, 128) for i in range(PAST // 128)] + [(PAST, NS)]
            for h in range(NH):
                for m in range(2):
                    hm = h * 2 + m
                    first = True
                    kidx = 0
                    nkt = len(ktl)
                    started = [False] * len(grp)
                    def visible_q0(k0):
                        if not seq.prompt or is_meta:
                            return 0
                        if k0 < g0:
                            return 0
                        return k0 - g0
                    last_for_t = []
                    for ti, (r0, n) in enumerate(grp):
                        lk = max(i for i, (k0, nk) in enumerate(ktl) if visible_q0(k0) <= r0 - g0)
                        last_for_t.append(lk)
                    while kidx < nkt:
                        c0 = ktl[kidx][0]
                        j = kidx
                        while j < nkt and ktl[j][0] + ktl[j][1] - c0 <= KCH:
                            j += 1
                        cn = ktl[j - 1][0] + ktl[j - 1][1] - c0
                        kt_ = kT[cnt["w1"] % 2]
                        vt_ = vt[cnt["w1"] % 2]
                        cnt["w1"] += 1
                        dma("sp", kt_[:, 0:cn], seq.KT[hm, :, c0:c0 + cn], R=[SCR], W=[kt_])
                        jj = kidx
                        while jj < j:
                            k0, nk = ktl[jj]
                            if nk == 128:
                                je = jj
                                while je < j and ktl[je][1] == 128:
                                    je += 1
                                cnt_t = je - jj
                                dma("sp", vt_[:, jj - kidx:je - kidx, 0:256], seq.V[k0:k0 + cnt_t * 128, h, :].rearrange("(j p) d -> p j d", p=128), R=[SCR], W=[vt_])
                                jj = je
                            else:
                                dma("sp", vt_[:nk, jj - kidx, 0:256], seq.V[k0:k0 + nk, h, :], R=[SCR], W=[vt_])
                                jj += 1
                        for jj in range(kidx, j):
                            k0, nk = ktl[jj]
                            q0 = visible_q0(k0)
                            sb_ = PB[4 + (cnt["pb"] % 2)]
                            pt_ = pT[cnt["pb"] % 3]
                            cnt["pb"] += 1
                            bias = None
                            if seq.prompt:
                                if is_meta:
                                    bias = W_mq[h][:nk, 0:gn]
                                elif k0 < NMETA:
                                    if g0 == NMETA:
                                        bias = W_m0[h][:nk, 0:gn]
                                elif k0 >= g0:
                                    bias = W_pr[h][:nk, 0:gn - q0]
                                elif k0 == g0 - 128:
                                    bias = W_pr[h][:nk, 128:128 + gn]
                            else:
                                if k0 == PAST - 128:
                                    bias = W_sp[h][:nk, 0:gn]
                                elif k0 == PAST:
                                    bias = W_sd[h][:nk, 0:gn]
                            mm(sb_[:nk, q0:gn], kt_[:, k0 - c0:k0 - c0 + nk], qT[:, hm, q0:gn], start=True, stop=(bias is None), R=[kt_, qT], W=[sb_])
                            if bias is not None:
                                mm(sb_[:nk, q0:gn], ident_b[:nk, :nk], bias, start=False, stop=True, R=[ident_b, W_pr[h], W_m0[h], W_mq[h], W_sp[h], W_sd[h]], W=[sb_])
                            op("act", lambda e: e.activation(out=pt_[:nk, q0:gn], in_=sb_[:nk, q0:gn], func=AF.Exp, scale=float(HD ** -0.5)), R=[sb_], W=[pt_])
                            for ti, (r0, n) in enumerate(grp):
                                c_lo = r0 - g0
                                if c_lo < q0:
                                    continue
                                ob = PB[ti]
                                mm(ob[:n, 0:257], pt_[:nk, c_lo:c_lo + n], vt_[:nk, jj - kidx, :], start=(not started[ti]), stop=(jj == last_for_t[ti]), R=[pt_, vt_], W=[ob])
                                started[ti] = True
                        kidx = j
                    for ti, (r0, n) in enumerate(grp):
                        ob = PB[ti]
                        op("dve", lambda e: e.reciprocal(rr[:n, ti * 2 + m:ti * 2 + m + 1], ob[:n, 256:257]), R=[ob], W=[rr])
                        if m == 0:
                            op("dve", lambda e: e.tensor_scalar(out=o0[:n, ti, :], in0=ob[:n, 0:256], scalar1=rr[:n, ti * 2:ti * 2 + 1], scalar2=None, op0=ALU.mult), R=[ob, rr], W=[o0])
                        else:
                            op("dve", lambda e: e.tensor_tensor(out=rr[:n, 8 + ti:9 + ti], in0=rr[:n, ti * 2 + 1:ti * 2 + 2], in1=nlam_bc[:n, :], op=ALU.mult), R=[rr, nlam_bc], W=[rr])
                            op("dve", lambda e: e.scalar_tensor_tensor(out=dd[:n, :], in0=ob[:n, 0:256], scalar=rr[:n, 8 + ti:9 + ti], in1=o0[:n, ti, :], op0=ALU.mult, op1=ALU.add), R=[ob, rr, o0], W=[dd])
                            op("act", lambda e: e.activation(out=dsq[:n, :], in_=dd[:n, :], func=AF.Square, accum_out=rr[:n, 12:13]), R=[dd], W=[dsq, rr])
                            op("dve", lambda e: e.tensor_scalar(out=rr[:n, 13:14], in0=rr[:n, 12:13], scalar1=1.0 / 256, scalar2=EPS, op0=ALU.mult, op1=ALU.add), R=[rr], W=[rr])
                            rsq(rr[:n, 13:14], rr)
                            op("dve", lambda e: e.scalar_tensor_tensor(out=yb[:n, :], in0=dd[:n, :], scalar=rr[:n, 13:14], in1=don_bc[:n, :], op0=ALU.mult, op1=ALU.mult), R=[dd, rr, don_bc], W=[yb])
                            pb = PB[6]
                            pv = pb.t[:].bitcast(BF16)
                            for hf in range(2):
                                tr(pv[:, hf * 128:hf * 128 + n], yb[:n, hf * 128:(hf + 1) * 128], ident_b[:n, :n], R=[yb, ident_b], W=[pb])
                            op("act", lambda e: e.activation(out=mixst[:, :, r0 - g0:r0 - g0 + n], in_=pv[:, 0:256].rearrange("p (c t) -> p c t", t=128)[:, :, 0:n], func=AF.Copy), R=[pb], W=[mixst])
                dma("sp", seq.MIXT[h * 256:(h + 1) * 256, g0:g0 + gn].rearrange("(c p) t -> p c t", p=128), mixst[:, :, 0:gn], R=[mixst], W=[SCR], sem=mixst)

    Sg = sb("Sg", [128, NH, 128])
    Cm = sb("Cm", [128, NH, 129])
    mprev = sb("mprev", [128, NH])
    zg = [sb("zg%d" % i, [128, 1536]) for i in range(4)]
    gacc = sb("gacc", [128, 1536])
    gtmp = sb("gtmp", [128, 1536])
    zr = sb("zr", [128, INC - O_GB])
    sm = sb("sm", [128, 64])
    gq = sb("gq", [128, NH, 128])
    gk = sb("gk", [128, NH, 128])
    gkT = sb("gkT", [128, NH, 128])
    gqT = sb("gqT", [128, NH, 128])
    big1 = sb("big1", [128, NH, 128])
    big2 = sb("big2", [128, NH, 128])
    decT = sb("decT", [128, NH, 128])
    XP = [sb("XP%d" % i, [128, NH, 128]) for i in range(2)]
    XPT = [sb("XPT%d" % i, [128, NH, 128]) for i in range(2)]
    RR_ = [sb("RR%d" % i, [128, NH, 128]) for i in range(2)]
    RRT = [sb("RRT%d" % i, [128, NH, 128]) for i in range(2)]
    qkd = sb("qkd", [128, NH, 128])
    ktil = sb("ktil", [128, NH, 128])
    rp = sb("rp", [128, NH, 128])
    uu = sb("uu", [128, NH, 128])
    og = sb("og", [128, NH, 128])
    om = sb("om", [128, NH, 128])
    mv1 = sb("mv1", [128, NH, 129])
    mq = sb("mq", [128, NH, 128])
    mk = sb("mk", [128, NH, 128])
    mqT = sb("mqT", [128, NH, 128])
    mkT = sb("mkT", [128, NH, 128])
    t4 = sb("t4", [NH, 128])
    t4b = sb("t4b", [NH, 128])
    mixb = sb("mixb", [128, 1024], BF16)
    mixT2 = sb("mixT2", [128, 8, 128], BF16)

    def recur(seq, l, st_out):
        gc_o, gS_o, mC_o, mn_o, mm_o = st_out
        if seq.prompt:
            op("pool", lambda e: e.memset(Sg[:], 0.0), W=[Sg])
            op("pool", lambda e: e.memset(Cm[:], 0.0), W=[Cm])
            op("pool", lambda e: e.memset(mprev[:], 0.0), W=[mprev])
        else:
            dma("sp", Sg[:], gS_in[l].rearrange("h k v -> k h v"), W=[Sg])
            dma("sp", Cm[:, :, 0:128], mC_in[l].rearrange("h k v -> k h v"), W=[Cm])
            dma("sp", Cm[:, :, 128:129], mn_in[l].rearrange("h (k o) -> k h o", o=1), W=[Cm])
            dma("sp", mprev[:], mm_in[l:l + 1, :].broadcast_to([128, NH]), W=[mprev])
        for (r0, n) in seq.tiles:
            sel_last = SEL[n]
            for i in range(4):
                dma("sp", zg[i][:n, :], zc(seq, slice(r0 + i, r0 + i + n), O_GQKV, O_GQKV + 1536), R=[SCR], W=[zg[i]])
            dma("sp", zr[:n, 0:5120 - O_GB], zc(seq, slice(3 + r0, 3 + r0 + n), O_GB, 5120), R=[SCR], W=[zr])
            dma("sp", zr[:n, 5120 - O_GB:INC - O_GB], zc(seq, slice(3 + r0, 3 + r0 + n), 5120, INC), R=[SCR], W=[zr])
            op("dve", lambda e: e.tensor_tensor(out=gacc[:n, :], in0=zg[0][:n, :], in1=gcw_bc[:n, 0, :], op=ALU.mult), R=[zg[0], gcw_bc], W=[gacc])
            for i in range(1, 4):
                eng = "pool" if i % 2 else "dve"
                op(eng, lambda e: e.tensor_tensor(out=gtmp[:n, :], in0=zg[i][:n, :], in1=gcw_bc[:n, i, :], op=ALU.mult), R=[zg[i], gcw_bc], W=[gtmp])
                op("dve", lambda e: e.tensor_tensor(out=gacc[:n, :], in0=gacc[:n, :], in1=gtmp[:n, :], op=ALU.add), R=[gacc, gtmp], W=[gacc])
            op("act", lambda e: e.activation(out=gacc[:n, :], in_=gacc[:n, :], func=AF.Silu), R=[gacc], W=[gacc])
            op("dve", lambda e: e.tensor_tensor(out=gtmp[:n, 0:1024], in0=gacc[:n, 0:1024], in1=gacc[:n, 0:1024], op=ALU.mult), R=[gacc], W=[gtmp])
            op("dve", lambda e: e.tensor_reduce(out=sm[:n, 0:8], in_=gtmp[:n, 0:1024].rearrange("p (b d) -> p b d", d=128), axis=AX.X, op=ALU.add), R=[gtmp], W=[sm])
            rsq(sm[:n, 0:8], sm, True)
            op("dve", lambda e: e.tensor_scalar_mul(sm[:n, 0:4], sm[:n, 0:4], float(HD ** -0.5)), R=[sm], W=[sm])
            op("dve", lambda e: e.tensor_tensor(out=gq[:n, :, :], in0=gacc[:n, 0:512].rearrange("p (b d) -> p b d", d=128), in1=sm[:n, 0:4, None].broadcast_to([n, 4, 128]), op=ALU.mult), R=[gacc, sm], W=[gq])
            op("dve", lambda e: e.tensor_tensor(out=gk[:n, :, :], in0=gacc[:n, 512:1024].rearrange("p (b d) -> p b d", d=128), in1=sm[:n, 4:8, None].broadcast_to([n, 4, 128]), op=ALU.mult), R=[gacc, sm], W=[gk])
            gv = gacc
            zb = zr
            op("act", lambda e: e.activation(out=sm[:n, 8:12], in_=zb[:n, 0:4], func=AF.Sigmoid), R=[zr], W=[sm])
            op("dve", lambda e: e.tensor_tensor(out=sm[:n, 12:16], in0=zb[:n, 4:8], in1=small_bc[:n, 4:8], op=ALU.add), R=[zr, small_bc], W=[sm])
            op("act", lambda e: e.activation(out=sm[:n, 12:16], in_=sm[:n, 12:16], func=AF.Exp), R=[sm], W=[sm])
            op("act", lambda e: e.activation(out=sm[:n, 12:16], in_=sm[:n, 12:16], func=AF.Ln, bias=1.0), R=[sm], W=[sm])
            op("dve", lambda e: e.tensor_tensor(out=sm[:n, 12:16], in0=sm[:n, 12:16], in1=negA_bc[:n, :], op=ALU.mult), R=[sm, negA_bc], W=[sm])
            op("dve", lambda e: e.tensor_tensor(out=sm[:n, 16:20], in0=zb[:n, O_MI - O_GB:O_MI - O_GB + 4], in1=small_bc[:n, 8:12], op=ALU.add), R=[zr, small_bc], W=[sm])
            op("dve", lambda e: e.tensor_tensor(out=sm[:n, 20:24], in0=zb[:n, O_MF - O_GB:O_MF - O_GB + 4], in1=small_bc[:n, 12:16], op=ALU.add), R=[zr, small_bc], W=[sm])
            op("act", lambda e: e.activation(out=sm[:n, 20:24], in_=sm[:n, 20:24], func=AF.Exp, scale=-1.0), R=[sm], W=[sm])
            op("act", lambda e: e.activation(out=sm[:n, 20:24], in_=sm[:n, 20:24], func=AF.Ln, bias=1.0), R=[sm], W=[sm])
            op("dve", lambda e: e.tensor_scalar_mul(sm[:n, 20:24], sm[:n, 20:24], -1.0), R=[sm], W=[sm])
            pb = PB[7]
            mm(pb[:n, 0:4], U_le[:n, :n], sm[:n, 12:16], R=[U_le, sm], W=[pb])
            mm(pb[:n, 4:8], U_le[:n, :n], sm[:n, 20:24], R=[U_le, sm], W=[pb])
            mm(pb[:, 8:12], ones_f[:n, :], sm[:n, 12:16], R=[ones_f, sm], W=[pb])
            mm(pb[:, 12:16], ones_f[:n, :], sm[:n, 20:24], R=[ones_f, sm], W=[pb])
            op("dve", lambda e: e.tensor_copy(sm[:, 24:40], pb[:, 0:16]), R=[pb], W=[sm])
            op("act", lambda e: e.activation(out=sm[:n, 40:44], in_=sm[:n, 24:28], func=AF.Exp), R=[sm], W=[sm])
            op("act", lambda e: e.activation(out=sm[:, 44:48], in_=sm[:, 32:36], func=AF.Exp), R=[sm], W=[sm])
            op("dve", lambda e: e.tensor_tensor(out=sm[:n, 48:52], in0=sm[:n, 32:36], in1=sm[:n, 24:28], op=ALU.subtract), R=[sm], W=[sm])
            op("act", lambda e: e.activation(out=sm[:n, 48:52], in_=sm[:n, 48:52], func=AF.Exp), R=[sm], W=[sm])
            for src, dst, pbi in ((gq, gqT, 4), (gk, gkT, 5)):
                pb = PB[pbi]
                for h in range(NH):
                    tr(pb[:, h * 128:h * 128 + n], src[:n, h, :], ident_f[:n, :n], R=[src, ident_f], W=[pb])
                op("act", lambda e: e.activation(out=dst[:, :, 0:n], in_=pb[:, :].rearrange("p (h t) -> p h t", t=128)[:, :, 0:n], func=AF.Copy), R=[pb], W=[dst])
            for h in range(NH):
                op("dve", lambda e: e.tensor_scalar(out=big1[:n, h, 0:n], in0=U_le[:n, 0:n], scalar1=sm[:n, 12 + h:13 + h], scalar2=None, op0=ALU.mult), R=[U_le, sm], W=[big1])
            pb = PB[4]
            for h in range(NH):
                mm(pb[:n, h * 128:h * 128 + n], U_gt[:n, :n], big1[:n, h, 0:n], R=[U_gt, big1], W=[pb])
            op("dve", lambda e: e.tensor_tensor(out=decT[:n, :, 0:n], in0=pb[:n, :].rearrange("p (h t) -> p h t", t=128)[:, :, 0:n], in1=negm_T[:n, None, 0:n].broadcast_to([n, NH, n]), op=ALU.add), R=[pb, negm_T], W=[decT])
            op("act", lambda e: e.activation(out=decT[:n, :, 0:n], in_=decT[:n, :, 0:n], func=AF.Exp), R=[decT], W=[decT])
            pb = PB[5]
            for h in range(NH):
                mm(pb[:n, h * 128:h * 128 + n], gkT[:, h, 0:n], gkT[:, h, 0:n], R=[gkT], W=[pb])
            op("dve", lambda e: e.tensor_tensor(out=big2[:n, :, 0:n], in0=pb[:n, :].rearrange("p (h t) -> p h t", t=128)[:, :, 0:n], in1=decT[:n, :, 0:n], op=ALU.mult), R=[pb, decT], W=[big2])
            op("dve", lambda e: e.tensor_tensor(out=big2[:n, :, 0:n], in0=big2[:n, :, 0:n], in1=m_strictT[:n, None, 0:n].broadcast_to([n, NH, n]), op=ALU.mult), R=[big2, m_strictT], W=[big2])
            op("dve", lambda e: e.tensor_tensor(out=XP[0][:n, :, 0:n], in0=big2[:n, :, 0:n], in1=sm[:n, 8:12, None].broadcast_to([n, NH, n]), op=ALU.mult), R=[big2, sm], W=[XP[0]])
            pb = PB[6]
            for h in range(NH):
                mm(pb[:n, h * 128:h * 128 + n], gkT[:, h, 0:n], gqT[:, h, 0:n], R=[gkT, gqT], W=[pb])
            op("dve", lambda e: e.tensor_tensor(out=qkd[:n, :, 0:n], in0=pb[:n, :].rearrange("p (h t) -> p h t", t=128)[:, :, 0:n], in1=decT[:n, :, 0:n], op=ALU.mult), R=[pb, decT], W=[qkd])
            pb = PB[4]
            for h in range(NH):
                tr(pb[:n, h * 128:h * 128 + n], XP[0][:n, h, 0:n], ident_f[:n, :n], R=[XP[0], ident_f], W=[pb])
            op("act", lambda e: e.activation(out=XPT[0][:n, :, 0:n], in_=pb[:n, :].rearrange("p (h t) -> p h t", t=128)[:, :, 0:n], func=AF.Copy), R=[pb], W=[XPT[0]])
            op("dve", lambda e: e.tensor_tensor(out=RR_[0][:n, :, 0:n], in0=ident_f[:n, None, 0:n].broadcast_to([n, NH, n]), in1=XP[0][:n, :, 0:n], op=ALU.subtract), R=[ident_f, XP[0]], W=[RR_[0]])
            op("pool", lambda e: e.tensor_tensor(out=RRT[0][:n, :, 0:n], in0=ident_f[:n, None, 0:n].broadcast_to([n, NH, n]), in1=XPT[0][:n, :, 0:n], op=ALU.subtract), R=[ident_f, XPT[0]], W=[RRT[0]])
            nlev = max(1, int(math.ceil(math.log2(n))) - 1)
            cur = 0
            for lev in range(nlev):
                nx = 1 - cur
                pa, pbk = PB[4], PB[5]
                for h in range(NH):
                    mm(pa[:n, h * 128:h * 128 + n], XPT[cur][:n, h, 0:n], XP[cur][:n, h, 0:n], R=[XPT[cur], XP[cur]], W=[pa])
                for h in range(NH):
                    mm(pbk[:n, h * 128:h * 128 + n], XP[cur][:n, h, 0:n], XPT[cur][:n, h, 0:n], R=[XPT[cur], XP[cur]], W=[pbk])
                op("dve", lambda e: e.tensor_copy(XP[nx][:n, :, 0:n], pa[:n, :].rearrange("p (h t) -> p h t", t=128)[:, :, 0:n]), R=[pa], W=[XP[nx]])
                op("act", lambda e: e.activation(out=XPT[nx][:n, :, 0:n], in_=pbk[:n, :].rearrange("p (h t) -> p h t", t=128)[:, :, 0:n], func=AF.Copy), R=[pbk], W=[XPT[nx]])
                pa, pbk = PB[6], PB[7]
                for h in range(NH):
                    mm(pa[:n, h * 128:h * 128 + n], RRT[cur][:n, h, 0:n], XP[nx][:n, h, 0:n], R=[RRT[cur], XP[nx]], W=[pa])
                for h in range(NH):
                    mm(pbk[:n, h * 128:h * 128 + n], XP[nx][:n, h, 0:n], RRT[cur][:n, h, 0:n], R=[RRT[cur], XP[nx]], W=[pbk])
                op("dve", lambda e: e.tensor_tensor(out=RR_[nx][:n, :, 0:n], in0=pa[:n, :].rearrange("p (h t) -> p h t", t=128)[:, :, 0:n], in1=RR_[cur][:n, :, 0:n], op=ALU.add), R=[pa, RR_[cur]], W=[RR_[nx]])
                op("dve", lambda e: e.tensor_tensor(out=RRT[nx][:n, :, 0:n], in0=pbk[:n, :].rearrange("p (h t) -> p h t", t=128)[:, :, 0:n], in1=RRT[cur][:n, :, 0:n], op=ALU.add), R=[pbk, RRT[cur]], W=[RRT[nx]])
                cur = nx
            TT = RR_[cur]
            op("dve", lambda e: e.tensor_tensor(out=ktil[:n, :, :], in0=gk[:n, :, :], in1=sm[:n, 48:52, None].broadcast_to([n, NH, 128]), op=ALU.mult), R=[gk, sm], W=[ktil])
            pa, pbk = PB[4], PB[5]
            for h in range(NH):
                mm(pa[:n, h * 128:(h + 1) * 128], gkT[:, h, 0:n], Sg[:, h, :], R=[gkT, Sg], W=[pa])
            for h in range(NH):
                mm(pbk[:n, h * 128:(h + 1) * 128], gqT[:, h, 0:n], Sg[:, h, :], R=[gqT, Sg], W=[pbk])
            op("dve", lambda e: e.tensor_tensor(out=rp[:n, :, :], in0=pa[:n, :].rearrange("p (h t) -> p h t", t=128), in1=sm[:n, 40:44, None].broadcast_to([n, NH, 128]), op=ALU.mult), R=[pa, sm], W=[rp])
            op("dve", lambda e: e.tensor_tensor(out=rp[:n, :, :], in0=gv[:n, 1024:1536].rearrange("p (h t) -> p h t", t=128), in1=rp[:n, :, :], op=ALU.subtract), R=[gacc, rp], W=[rp])
            op("dve", lambda e: e.tensor_tensor(out=og[:n, :, :], in0=pbk[:n, :].rearrange("p (h t) -> p h t", t=128), in1=sm[:n, 40:44, None].broadcast_to([n, NH, 128]), op=ALU.mult), R=[pbk, sm], W=[og])
            pa = PB[6]
            for h in range(NH):
                mm(pa[:n, h * 128:(h + 1) * 128], TT[:n, h, 0:n], rp[:n, h, :], R=[TT, rp], W=[pa])
            op("dve", lambda e: e.tensor_tensor(out=uu[:n, :, :], in0=pa[:n, :].rearrange("p (h t) -> p h t", t=128), in1=sm[:n, 8:12, None].broadcast_to([n, NH, 128]), op=ALU.mult), R=[pa, sm], W=[uu])
            pa, pbk = PB[4], PB[5]
            for h in range(NH):
                mm(pa[:n, h * 128:(h + 1) * 128], qkd[:n, h, 0:n], uu[:n, h, :], R=[qkd, uu], W=[pa])
            for h in range(NH):
                mm(pbk[:, h * 128:(h + 1) * 128], ktil[:n, h, :], uu[:n, h, :], R=[ktil, uu], W=[pbk])
            op("dve", lambda e: e.tensor_tensor(out=og[:n, :, :], in0=pa[:n, :].rearrange("p (h t) -> p h t", t=128), in1=og[:n, :, :], op=ALU.add), R=[pa, og], W=[og])
            op("dve", lambda e: e.tensor_tensor(out=Sg[:, :, :], in0=Sg[:, :, :], in1=sm[:, 44:48, None].broadcast_to([128, NH, 128]), op=ALU.mult), R=[Sg, sm], W=[Sg])
            op("dve", lambda e: e.tensor_tensor(out=Sg[:, :, :], in0=pbk[:, :].rearrange("p (h t) -> p h t", t=128), in1=Sg[:, :, :], op=ALU.add), R=[pbk, Sg], W=[Sg])
            op("dve", lambda e: e.tensor_tensor(out=big1[:n, :, :], in0=og[:n, :, :], in1=og[:n, :, :], op=ALU.mult), R=[og], W=[big1])
            op("dve", lambda e: e.tensor_reduce(out=sm[:n, 52:56], in_=big1[:n, :, :], axis=AX.X, op=ALU.add), R=[big1], W=[sm])
            op("dve", lambda e: e.tensor_scalar(out=sm[:n, 52:56], in0=sm[:n, 52:56], scalar1=1.0 / 128, scalar2=EPS, op0=ALU.mult, op1=ALU.add), R=[sm], W=[sm])
            rsq(sm[:n, 52:56], sm)
            op("dve", lambda e: e.tensor_tensor(out=og[:n, :, :], in0=og[:n, :, :], in1=sm[:n, 52:56, None].broadcast_to([n, NH, 128]), op=ALU.mult), R=[og, sm], W=[og])
            op("pool", lambda e: e.tensor_tensor(out=og[:n, :, :], in0=og[:n, :, :], in1=gon_bc[:n, None, :].broadcast_to([n, NH, 128]), op=ALU.mult), R=[og, gon_bc], W=[og])
            op("act", lambda e: e.activation(out=big1[:n, :, :].rearrange("p h t -> p (h t)"), in_=zb[:n, O_GG - O_GB:O_GG - O_GB + 512], func=AF.Silu), R=[zr], W=[big1])
            op("dve", lambda e: e.tensor_tensor(out=mixb[:n, 0:512], in0=og[:n, :, :].rearrange("p h t -> p (h t)"), in1=big1[:n, :, :].rearrange("p h t -> p (h t)"), op=ALU.mult), R=[og, big1], W=[mixb])

            zq = O_MQKV - O_GB
            op("act", lambda e: e.activation(out=mq[:n, :, :].rearrange("p h t -> p (h t)"), in_=zb[:n, zq:zq + 512], func=AF.Copy), R=[zr], W=[mq])
            op("act", lambda e: e.activation(out=mk[:n, :, :].rearrange("p h t -> p (h t)"), in_=zb[:n, zq + 512:zq + 1024], func=AF.Copy, scale=float(HD ** -0.5)), R=[zr], W=[mk])
            op("pool", lambda e: e.tensor_copy(mv1[:n, :, 0:128], zb[:n, zq + 1024:zq + 1536].rearrange("p (h t) -> p h t", t=128)), R=[zr], W=[mv1])
            op("pool", lambda e: e.memset(mv1[:n, :, 128:129], 1.0), W=[mv1])
            for src, dst, pbi in ((mq, mqT, 4), (mk, mkT, 5)):
                pb = PB[pbi]
                for h in range(NH):
                    tr(pb[:, h * 128:h * 128 + n], src[:n, h, :], ident_f[:n, :n], R=[src, ident_f], W=[pb])
                op("act", lambda e: e.activation(out=dst[:, :, 0:n], in_=pb[:, :].rearrange("p (h t) -> p h t", t=128)[:, :, 0:n], func=AF.Copy), R=[pb], W=[dst])
            op("dve", lambda e: e.tensor_tensor(out=sm[:n, 56:60], in0=sm[:n, 16:20], in1=sm[:n, 28:32], op=ALU.subtract), R=[sm], W=[sm])
            pb = PB[7]
            tr(pb[:NH, 0:n], sm[:n, 56:60], ident_f[:n, :n], R=[sm, ident_f], W=[pb])
            tr(pb[:NH, 128:256], mprev[:, :], ident_f[:, :], R=[mprev, ident_f], W=[pb])
            op("dve", lambda e: e.tensor_copy(t4[:, 0:n], pb[:NH, 0:n]), R=[pb], W=[t4])
            op("dve", lambda e: e.tensor_copy(t4b[:, 0:1], pb[:NH, 128:129]), R=[pb], W=[t4b])
            op("dve", lambda e: e.tensor_tensor_scan(out=t4[:, 0:n], data0=t4[:, 0:n], data1=t4[:, 0:n], initial=t4b[:, 0:1], op0=ALU.max, op1=ALU.max), R=[t4, t4b], W=[t4])
            tr(pb[:n, 256:260], t4[:, 0:n], ident_f[:NH, :NH], R=[t4, ident_f], W=[pb])
            op("dve", lambda e: e.tensor_copy(sm[:n, 56:60], pb[:n, 256:260]), R=[pb], W=[sm])
            op("dve", lambda e: e.tensor_tensor(out=sm[:n, 60:64], in0=sm[:n, 56:60], in1=sm[:n, 28:32], op=ALU.add), R=[sm], W=[sm])
            for h in range(NH):
                op("dve", lambda e: e.tensor_scalar(out=big1[:n, h, 0:n], in0=U_le[:n, 0:n], scalar1=sm[:n, 20 + h:21 + h], scalar2=None, op0=ALU.mult), R=[U_le, sm], W=[big1])
                op("dve", lambda e: e.tensor_scalar(out=big2[:n, h, 0:n], in0=ident_f[:n, 0:n], scalar1=sm[:n, 16 + h:17 + h], scalar2=None, op0=ALU.mult), R=[ident_f, sm], W=[big2])
            pb = PB[4]
            for h in range(NH):
                mm(pb[:n, h * 128:h * 128 + n], big1[:n, h, 0:n], U_gt[:n, 0:n], start=True, stop=False, R=[big1, U_gt], W=[pb])
                mm(pb[:n, h * 128:h * 128 + n], ones_f[:n, 0:n], big2[:n, h, 0:n], start=False, stop=True, R=[ones_f, big2], W=[pb])
            op("dve", lambda e: e.tensor_tensor(out=decT[:n, :, 0:n], in0=pb[:n, :].rearrange("p (h t) -> p h t", t=128)[:, :, 0:n], in1=negm[:n, None, 0:n].broadcast_to([n, NH, n]), op=ALU.add), R=[pb, negm], W=[decT])
            op("dve", lambda e: e.tensor_tensor(out=decT[:n, :, 0:n], in0=decT[:n, :, 0:n], in1=sm[:n, 60:64, None].broadcast_to([n, NH, n]), op=ALU.subtract), R=[decT, sm], W=[decT])
            op("act", lambda e: e.activation(out=decT[:n, :, 0:n], in_=decT[:n, :, 0:n], func=AF.Exp), R=[decT], W=[decT])
            pb = PB[5]
            for h in range(NH):
                mm(pb[:n, h * 128:h * 128 + n], mqT[:, h, 0:n], mkT[:, h, 0:n], R=[mqT, mkT], W=[pb])
            op("dve", lambda e: e.tensor_tensor(out=big1[:n, :, 0:n], in0=pb[:n, :].rearrange("p (h t) -> p h t", t=128)[:, :, 0:n], in1=decT[:n, :, 0:n], op=ALU.mult), R=[pb, decT], W=[big1])
            op("dve", lambda e: e.tensor_reduce(out=sm[:n, 52:56], in_=big1[:n, :, 0:n], axis=AX.X, op=ALU.add), R=[big1], W=[sm])
            pb = PB[6]
            for h in range(NH):
                tr(pb[:n, h * 128:h * 128 + n], big1[:n, h, 0:n], ident_f[:n, :n], R=[big1, ident_f], W=[pb])
            op("act", lambda e: e.activation(out=big2[:n, :, 0:n], in_=pb[:n, :].rearrange("p (h t) -> p h t", t=128)[:, :, 0:n], func=AF.Copy), R=[pb], W=[big2])
            op("dve", lambda e: e.tensor_tensor(out=sm[:n, 0:4], in0=sm[:n, 28:32], in1=mprev[:n, :], op=ALU.add), R=[sm, mprev], W=[sm])
            op("dve", lambda e: e.tensor_tensor(out=sm[:n, 0:4], in0=sm[:n, 0:4], in1=sm[:n, 60:64], op=ALU.subtract), R=[sm], W=[sm])
            op("act", lambda e: e.activation(out=sm[:n, 0:4], in_=sm[:n, 0:4], func=AF.Exp), R=[sm], W=[sm])
            pa, pbk = PB[4], PB[5]
            pn = PB[7]
            for h in range(NH):
                mm(pa[:n, h * 128:(h + 1) * 128], mqT[:, h, 0:n], Cm[:, h, 0:128], R=[mqT, Cm], W=[pa])
            for h in range(NH):
                mm(pn[:n, h:h + 1], mqT[:, h, 0:n], Cm[:, h, 128:129], R=[mqT, Cm], W=[pn])
            for h in range(NH):
                mm(pbk[:n, h * 128:(h + 1) * 128], big2[:n, h, 0:n], mv1[:n, h, 0:128], R=[big2, mv1], W=[pbk])
            qc = pa[:n, :].rearrange("p (h t) -> p h t", t=128)
            op("dve", lambda e: e.tensor_tensor(out=rp[:n, :, :], in0=qc, in1=sm[:n, 0:4, None].broadcast_to([n, NH, 128]), op=ALU.mult), R=[pa, sm], W=[rp])
            op("dve", lambda e: e.tensor_tensor(out=sm[:n, 4:8], in0=pn[:n, 0:4], in1=sm[:n, 0:4], op=ALU.mult), R=[pn, sm], W=[sm])
            op("dve", lambda e: e.tensor_tensor(out=om[:n, :, :], in0=pbk[:n, :].rearrange("p (h t) -> p h t", t=128), in1=rp[:n, :, :], op=ALU.add), R=[pbk, rp], W=[om])
            op("dve", lambda e: e.tensor_tensor(out=sm[:n, 4:8], in0=sm[:n, 4:8], in1=sm[:n, 52:56], op=ALU.add), R=[sm], W=[sm])
            op("act", lambda e: e.activation(out=sm[:n, 4:8], in_=sm[:n, 4:8], func=AF.Abs), R=[sm], W=[sm])
            op("act", lambda e: e.activation(out=sm[:n, 8:12], in_=sm[:n, 60:64], func=AF.Exp, scale=-1.0), R=[sm], W=[sm])
            op("dve", lambda e: e.tensor_tensor(out=sm[:n, 4:8], in0=sm[:n, 4:8], in1=sm[:n, 8:12], op=ALU.max), R=[sm], W=[sm])
            op("dve", lambda e: e.reciprocal(sm[:n, 4:8], sm[:n, 4:8]), R=[sm], W=[sm])
            op("dve", lambda e: e.tensor_tensor(out=om[:n, :, :], in0=om[:n, :, :], in1=sm[:n, 4:8, None].broadcast_to([n, NH, 128]), op=ALU.mult), R=[om, sm], W=[om])
            pb = PB[7]
            mm(pb[:, 0:4], sel_last[:n, :], sm[:n, 60:64], R=[sel_last, sm], W=[pb])
            op("dve", lambda e: e.tensor_copy(sm[:, 8:12], pb[:, 0:4]), R=[pb], W=[sm])
            op("dve", lambda e: e.tensor_tensor(out=sm[:, 12:16], in0=sm[:, 36:40], in1=mprev[:, :], op=ALU.add), R=[sm, mprev], W=[sm])
            op("dve", lambda e: e.tensor_tensor(out=sm[:, 12:16], in0=sm[:, 12:16], in1=sm[:, 8:12], op=ALU.subtract), R=[sm], W=[sm])
            op("act", lambda e: e.activation(out=sm[:, 12:16], in_=sm[:, 12:16], func=AF.Exp), R=[sm], W=[sm])
            op("dve", lambda e: e.tensor_tensor(out=sm[:n, 40:44], in0=sm[:n, 36:40], in1=sm[:n, 28:32], op=ALU.subtract), R=[sm], W=[sm])
            op("dve", lambda e: e.tensor_tensor(out=sm[:n, 40:44], in0=sm[:n, 40:44], in1=sm[:n, 16:20], op=ALU.add), R=[sm], W=[sm])
            op("dve", lambda e: e.tensor_tensor(out=sm[:n, 40:44], in0=sm[:n, 40:44], in1=sm[:n, 8:12], op=ALU.subtract), R=[sm], W=[sm])
            op("act", lambda e: e.activation(out=sm[:n, 40:44], in_=sm[:n, 40:44], func=AF.Exp), R=[sm], W=[sm])
            op("dve", lambda e: e.tensor_tensor(out=ktil[:n, :, :], in0=mk[:n, :, :], in1=sm[:n, 40:44, None].broadcast_to([n, NH, 128]), op=ALU.mult), R=[mk, sm], W=[ktil])
            pa = PB[6]
            pn = PB[7]
            for h in range(NH):
                mm(pa[:, h * 128:(h + 1) * 128], ktil[:n, h, :], mv1[:n, h, 0:128], R=[ktil, mv1], W=[pa])
            for h in range(NH):
                mm(pn[:, 8 + h:9 + h], ktil[:n, h, :], mv1[:n, h, 128:129], R=[ktil, mv1], W=[pn])
            op("dve", lambda e: e.tensor_tensor(out=Cm[:, :, :], in0=Cm[:, :, :], in1=sm[:, 12:16, None].broadcast_to([128, NH, 129]), op=ALU.mult), R=[Cm, sm], W=[Cm])
            op("dve", lambda e: e.tensor_tensor(out=Cm[:, :, 0:128], in0=pa[:, :].rearrange("p (h t) -> p h t", t=128), in1=Cm[:, :, 0:128], op=ALU.add), R=[pa, Cm], W=[Cm])
            op("dve", lambda e: e.tensor_tensor(out=Cm[:, :, 128], in0=pn[:, 8:12], in1=Cm[:, :, 128], op=ALU.add), R=[pn, Cm], W=[Cm])
            op("dve", lambda e: e.tensor_copy(mprev[:, :], sm[:, 8:12]), R=[sm], W=[mprev])
            op("dve", lambda e: e.tensor_tensor(out=big1[:n, :, :], in0=om[:n, :, :], in1=om[:n, :, :], op=ALU.mult), R=[om], W=[big1])
            op("dve", lambda e: e.tensor_reduce(out=sm[:n, 52:56], in_=big1[:n, :, :], axis=AX.X, op=ALU.add), R=[big1], W=[sm])
            op("dve", lambda e: e.tensor_scalar(out=sm[:n, 52:56], in0=sm[:n, 52:56], scalar1=1.0 / 128, scalar2=EPS, op0=ALU.mult, op1=ALU.add), R=[sm], W=[sm])
            rsq(sm[:n, 52:56], sm)
            op("dve", lambda e: e.tensor_tensor(out=om[:n, :, :], in0=om[:n, :, :], in1=sm[:n, 52:56, None].broadcast_to([n, NH, 128]), op=ALU.mult), R=[om, sm], W=[om])
            op("pool", lambda e: e.tensor_tensor(out=om[:n, :, :], in0=om[:n, :, :], in1=mon_bc[:n, None, :].broadcast_to([n, NH, 128]), op=ALU.mult), R=[om, mon_bc], W=[om])
            op("act", lambda e: e.activation(out=big1[:n, :, :].rearrange("p h t -> p (h t)"), in_=zb[:n, O_MO - O_GB:O_MO - O_GB + 512], func=AF.Sigmoid), R=[zr], W=[big1])
            op("dve", lambda e: e.tensor_tensor(out=mixb[:n, 512:1024], in0=om[:n, :, :].rearrange("p h t -> p (h t)"), in1=big1[:n, :, :].rearrange("p h t -> p (h t)"), op=ALU.mult), R=[om, big1], W=[mixb])
            for b4 in range(2):
                pb = PB[4 + b4]
                pv = pb.t[:].bitcast(BF16)
                for bb in range(4):
                    b = b4 * 4 + bb
                    tr(pv[:, bb * 128:bb * 128 + n], mixb[:n, b * 128:(b + 1) * 128], ident_b[:n, :n], R=[mixb, ident_b], W=[pb])
                op("act", lambda e: e.activation(out=mixT2[:, b4 * 4:(b4 + 1) * 4, 0:n], in_=pv[:, 0:512].rearrange("p (c t) -> p c t", t=128)[:, :, 0:n], func=AF.Copy), R=[pb], W=[mixT2])
            dma("sp", seq.MIXT[1024:2048, r0:r0 + n].rearrange("(c p) t -> p c t", p=128), mixT2[:, :, 0:n], R=[mixT2], W=[SCR], sem=mixT2)
        dma("sp", gS_o[l].rearrange("h k v -> k h v"), Sg[:], R=[Sg], W=[OUT], sem=Sg)
        dma("sp", mC_o[l].rearrange("h k v -> k h v"), Cm[:, :, 0:128], R=[Cm], W=[OUT], sem=Cm)
        dma("sp", mn_o[l].rearrange("h (k o) -> k h o", o=1), Cm[:, :, 128:129], R=[Cm], W=[OUT], sem=Cm)
        dma("sp", mm_o[l:l + 1, :], mprev[0:1, :], R=[mprev], W=[OUT], sem=mprev)
        dma("sp", zg[0][0:3, :], zc(seq, slice(seq.n, seq.n + 3), O_GQKV, O_GQKV + 1536), R=[SCR], W=[zg[0]])
        dma("sp", gc_o[l], zg[0][0:3, :], R=[zg[0]], W=[OUT], sem=zg[0])

    xres = sb("xres", [128, 4, D])
    w5 = [sb("w5_%d" % i, [128, 16, 512], BF16) for i in range(1)]
    wu = [sb("wu%d" % i, [128, 16, 128], BF16) for i in range(4)]
    hT = sb("hT", [128, 44, 512], BF16)
    gpre = sb("gpre", [128, 514])
    gcv = sb("gcv", [128, 512])
    halo = sb("halo", [128, 44, 2])
    wd = [sb("wd%d" % i, [128, 11, 512], BF16) for i in range(2)]
    fcst = sb("fcst", [128, 44, 2])

    def pass5(seq, l, xsrc, xdst_fn, fc_o):
        if seq.prompt:
            op("pool", lambda e: e.memset(halo[:], 0.0), W=[halo])
        else:
            rows_to_fm(fconv_in[l], 2, halo)
        for grp in groups(seq):
            g0 = grp[0][0]
            gn = sum(n for _, n in grp)
            dma("sp", hT[:, 0:16, 0:gn], seq.MIXT[:, g0:g0 + gn].rearrange("(c p) t -> p c t", p=128), R=[SCR], W=[hT])
            for ti, (r0, n) in enumerate(grp):
                dma("sp", xres[:n, ti, :], xsrc[r0:r0 + n, :], R=[SCR], W=[xres])
            for cb in range(4):
                wt = w5[0]
                cnt["w1"] += 1
                dma("sp", wt[:], Wb_out[cb], R=[SCR], W=[wt])
                for ti, (r0, n) in enumerate(grp):
                    pb = PB[ti]
                    for c in range(16):
                        mm(pb[:n, :], hT[:, c, ti * 128:ti * 128 + n], wt[:, c, :], start=(c == 0), stop=(c == 15), R=[hT, wt], W=[pb])
                    op("dve", lambda e: e.tensor_tensor(out=xres[:n, ti, cb * 512:(cb + 1) * 512], in0=pb[:n, :], in1=xres[:n, ti, cb * 512:(cb + 1) * 512], op=ALU.add), R=[pb, xres], W=[xres])
            for ti, (r0, n) in enumerate(grp):
                rms_to_T(SliceT(xres, ti), n, gffn_bc, xT, ti * 128, 4)
            for j in range(44):
                wg = wu[cnt["w1"] % 4]
                cnt["w1"] += 1
                wv_ = wu[cnt["w1"] % 4]
                cnt["w1"] += 1
                dma("sp", wg[:], Wb_up[j], R=[SCR], W=[wg])
                dma("sp", wv_[:], Wb_up[44 + j], R=[SCR], W=[wv_])
                pg = PB[4 + (j % 2) * 2]
                pvv = PB[5 + (j % 2) * 2]
                for c in range(16):
                    mm(pg[:, 0:gn], wg[:, c, :], xT[:, c, 0:gn], start=(c == 0), stop=(c == 15), R=[wg, xT], W=[pg])
                for c in range(16):
                    mm(pvv[:, 0:gn], wv_[:, c, :], xT[:, c, 0:gn], start=(c == 0), stop=(c == 15), R=[wv_, xT], W=[pvv])
                op("pool", lambda e: e.tensor_copy(gpre[:, 0:2], halo[:, j, :]), R=[halo], W=[gpre])
                op("act", lambda e: e.activation(out=gpre[:, 2:2 + gn], in_=pg[:, 0:gn], func=AF.Copy), R=[pg], W=[gpre])
                op("pool", lambda e: e.tensor_copy(halo[:, j, :], gpre[:, gn:gn + 2]), R=[gpre], W=[halo])
                op("dve", lambda e: e.tensor_scalar(out=gcv[:, 0:gn], in0=gpre[:, 0:gn], scalar1=fcw[:, j, 0:1], scalar2=None, op0=ALU.mult), R=[gpre, fcw], W=[gcv])
                op("dve", lambda e: e.scalar_tensor_tensor(out=gcv[:, 0:gn], in0=gpre[:, 1:1 + gn], scalar=fcw[:, j, 1:2], in1=gcv[:, 0:gn], op0=ALU.mult, op1=ALU.add), R=[gpre, fcw, gcv], W=[gcv])
                op("dve", lambda e: e.scalar_tensor_tensor(out=gcv[:, 0:gn], in0=gpre[:, 2:2 + gn], scalar=fcw[:, j, 2:3], in1=gcv[:, 0:gn], op0=ALU.mult, op1=ALU.add), R=[gpre, fcw, gcv], W=[gcv])
                op("act", lambda e: e.activation(out=gcv[:, 0:gn], in_=gcv[:, 0:gn], func=AF.Silu), R=[gcv], W=[gcv])
                op("dve", lambda e: e.tensor_tensor(out=hT[:, j, 0:gn], in0=gcv[:, 0:gn], in1=pvv[:, 0:gn], op=ALU.mult), R=[gcv, pvv], W=[hT])
            for cb in range(4):
                for qtr in range(4):
                    wt = wd[cnt["w1"] % 2]
                    cnt["w1"] += 1
                    dma("sp", wt[:], Wb_down[cb][:, qtr * 11:(qtr + 1) * 11, :], R=[SCR], W=[wt])
                    for ti, (r0, n) in enumerate(grp):
                        pb = PB[ti]
                        for c in range(11):
                            mm(pb[:n, :], hT[:, qtr * 11 + c, ti * 128:ti * 128 + n], wt[:, c, :], start=(qtr == 0 and c == 0), stop=(qtr == 3 and c == 10), R=[hT, wt], W=[pb])
                for ti, (r0, n) in enumerate(grp):
                    pb = PB[ti]
                    op("dve", lambda e: e.tensor_tensor(out=xres[:n, ti, cb * 512:(cb + 1) * 512], in0=pb[:n, :], in1=xres[:n, ti, cb * 512:(cb + 1) * 512], op=ALU.add), R=[pb, xres], W=[xres])
            for ti, (r0, n) in enumerate(grp):
                xdst_fn(r0, n, xres, ti)
        fm_to_rows(halo, 2, fc_o[l], [OUT])

    class SliceT:
        def __init__(self, base, ti):
            self.base = base
            self.ti = ti
            self.b = base.b

        def __getitem__(self, k):
            return self.base.t[k[0], self.ti, k[1]]

    for t_ in (ident_f, ident_b, ones_f, ones_b, U_le, U_gt, negm_T, negm, m_strictT, zero_f):
        t_._ensure()
    for t_ in W_pr + W_m0 + W_mq + W_sp + W_sd + [rb_bc, rbd_bc]:
        t_._ensure()
    scoped(setup_bias)
    for l in range(int(os.environ.get('KLAYERS', '2'))):
        lam_init = 0.8 - 0.6 * math.exp(-0.3 * l)
        scoped(lambda: precast(l))
        for t_ in (gmix_bc, gffn_bc, qn_bc, kn_bc, don_bc, gon_bc, mon_bc, small_bc, negA_bc, dal, lam_s, nlam_bc, fcw):
            t_._ensure()

        def s1():
            load_params(l, lam_init)
            for seq in (P, Q):
                xsrc = seq.x0 if l == 0 else seq.X1
                pass1(seq, l, xsrc)
            op("pool", lambda e: e.memset(zev[0][0:3, :], 0.0), W=[zev[0]])
            for c3 in range(3):
                dma("sp", zc(P, slice(0, 3), O_GQKV + c3 * 512, O_GQKV + (c3 + 1) * 512), zev[0][0:3, :], R=[zev[0]], W=[SCR], sem=zev[0])
            for c3 in range(3):
                dma("sp", zev[1 + c3][0:3, :], gconv_in[l][:, c3 * 512:(c3 + 1) * 512], W=[zev[1 + c3]])
                dma("sp", zc(Q, slice(0, 3), O_GQKV + c3 * 512, O_GQKV + (c3 + 1) * 512), zev[1 + c3][0:3, :], R=[zev[1 + c3]], W=[SCR], sem=zev[1 + c3])
        scoped(s1)

        def s2():
            pass2(P, l, k_p, v_p)
            cache_prep(l)
            pass2(Q, l, k_s, v_s)
        scoped(s2)

        def s3():
            attend(P, l)
            attend(Q, l)
        scoped(s3)

        def s4():
            for i in range(4):
                dma("sp", gcw_bc[:, i, :], gdn_conv_w[l, i:i + 1, :].broadcast_to([128, 1536]), W=[gcw_bc])
            recur(P, l, (gc_p, gS_p, mC_p, mn_p, mm_p))
            recur(Q, l, (gc_s, gS_s, mC_s, mn_s, mm_s))
        scoped(s4)

        def s5():
            for seq, yout, fco in ((P, y_p, fc_p), (Q, y_s, fc_s)):
                xsrc = seq.x0 if l == 0 else seq.X1
                if l == 0:
                    def wr(r0, n, xr, ti, _seq=seq):
                        dma("sp", _seq.X1[r0:r0 + n, :], xr[:n, ti, :], R=[xr], W=[SCR], sem=xr)
                elif seq.prompt:
                    def wr(r0, n, xr, ti, _y=yout):
                        if r0 >= NMETA:
                            dma("sp", _y[r0 - NMETA:r0 - NMETA + n, :], xr[:n, ti, :], R=[xr], W=[OUT], sem=xr)
                else:
                    def wr(r0, n, xr, ti, _y=yout):
                        dma("sp", _y[r0:r0 + n, :], xr[:n, ti, :], R=[xr], W=[OUT], sem=xr)
                pass5(seq, l, xsrc, wr, fco)
        scoped(s5)
    S.barrier()
    st.close()
    return nc, S


_CACHE = {}


def kernel(**inp):
    x_prompt = np.asarray(inp["x_prompt"], np.float32)
    B, SEQ, _ = x_prompt.shape
    NT = SEQ // 128
    PAST = inp["cache_attn_k"].shape[2]
    key = (NT, PAST)
    if key not in _CACHE:
        _CACHE[key] = build(NT, PAST)
    nc, _ = _CACHE[key]
    f = lambda k: np.ascontiguousarray(np.asarray(inp[k], np.float32))
    meta = f("meta_tokens")
    xs = f("x_sample")
    ck = f("cache_attn_k")
    cv = f("cache_attn_v")
    n_cores = 8
    shared = {k: f(k) for k in ["rel_bias", "norm_mix", "norm_ffn", "w_in", "w_out", "da_q_norm", "da_k_norm", "da_out_norm",
                                "gdn_conv_w", "gdn_A_log", "gdn_dt_bias", "gdn_out_norm", "ml_i_bias", "ml_f_bias", "ml_out_norm",
                                "ffn_w_up", "ffn_conv_w", "ffn_w_down"]}
    shared["da_l"] = np.ascontiguousarray(np.stack([f("da_lq1"), f("da_lk1"), f("da_lq2"), f("da_lk2")], axis=1))
    in_maps = []
    for c in range(n_cores):
        bp = c % B
        m = dict(shared)
        m["xp"] = np.ascontiguousarray(np.concatenate([meta, x_prompt[bp]], axis=0))
        m["xs"] = np.ascontiguousarray(xs[c])
        m["ck"] = np.ascontiguousarray(ck[:, c].reshape(2, PAST, 1024))
        m["cv"] = np.ascontiguousarray(cv[:, c].reshape(2, PAST, 1024))
        m["gconv"] = np.ascontiguousarray(f("state_gdn_conv")[:, c])
        m["gS"] = np.ascontiguousarray(f("state_gdn_S")[:, c])
        m["mC"] = np.ascontiguousarray(f("state_mlstm_C")[:, c])
        m["mn"] = np.ascontiguousarray(f("state_mlstm_n")[:, c])
        m["mm"] = np.ascontiguousarray(f("state_mlstm_m")[:, c])
        m["fconv"] = np.ascontiguousarray(f("state_ffn_conv")[:, c])
        in_maps.append(m)
    res = run_bass_kernel_spmd(nc, in_maps, core_ids=list(range(n_cores))).results
    NP = NMETA + SEQ
    pc = lambda name: np.stack([res[b][name] for b in range(B)], axis=1)
    sc = lambda name: np.stack([res[c][name] for c in range(n_cores)], axis=1)
    y_prompt = np.stack([res[b]["y_p"] for b in range(B)], axis=0)
    y_sample = np.stack([res[c]["y_s"] for c in range(n_cores)], axis=0)
    outs = (y_prompt, y_sample,
            pc("k_p").reshape(2, B, NP, NH, 2, 128), pc("v_p").reshape(2, B, NP, NH, 256),
            pc("gc_p"), pc("gS_p"), pc("mC_p"), pc("mn_p"), pc("mm_p"), pc("fc_p"),
            sc("k_s").reshape(2, n_cores, 64, NH, 2, 128), sc("v_s").reshape(2, n_cores, 64, NH, 256),
            sc("gc_s"), sc("gS_s"), sc("mC_s"), sc("mn_s"), sc("mm_s"), sc("fc_s"))
    return tuple(np.ascontiguousarray(o.astype(np.float32)) for o in outs)
```
